# Optimizing a Trainium2 kernel written in Bass

```python
import math
import jax
import jax.numpy as jnp
from jax import lax
import numpy as np

D_MODEL = 2048
BATCH = 32
SEQ = 256
DEPTH = 2
DEC_BATCH = 2
DEC_SEQ = 4096
PAST_LEN = 512

GRID_W = 64
N_MOD = 9
D_FF = 5504
NORM_EPS = 1e-6
NEG_INF = -1e30

GLA_HEADS = 4
GLA_DK = 128
GLA_DV = 256
GLA_RANK = 16
GLA_TAU = 16.0
GLA_CHUNK = 64

DN_HEADS = 8
DN_DK = 128
DN_DV = 128
DN_CONV = 5
DN_CHUNK = 64

SWA_Q_HEADS = 8
SWA_KV_HEADS = 2
SWA_GROUP = SWA_Q_HEADS // SWA_KV_HEADS
SWA_HD = 128
SWA_WINDOW = 128
SWA_BLOCK = 128
ROPE_BASE = 10000.0

N_BRANCH = 3
BRANCH_W = 1024

IN_SPLITS = (
    GLA_HEADS * GLA_DK, GLA_HEADS * GLA_DK, GLA_HEADS * GLA_DV, GLA_HEADS * GLA_DV, GLA_RANK, GLA_RANK,
    DN_HEADS * DN_DK, DN_HEADS * DN_DK, DN_HEADS * DN_DV, DN_HEADS, DN_HEADS, DN_HEADS, DN_HEADS, DN_HEADS * DN_DV,
    SWA_Q_HEADS * SWA_HD, SWA_KV_HEADS * SWA_HD, SWA_KV_HEADS * SWA_HD,
    N_BRANCH * D_MODEL,
)
D_IN = sum(IN_SPLITS)

kernel_name = 'hybrid_diffusion_gla_deltanet_swa_step'


def _rms(x, g):
    x32 = x.astype(jnp.float32)
    y = x32 * lax.rsqrt(jnp.mean(x32 * x32, axis=-1, keepdims=True) + NORM_EPS)
    return (y * g.astype(jnp.float32)).astype(x.dtype)


def _head_rms(o, g):
    return o * lax.rsqrt(jnp.mean(o * o, axis=-1, keepdims=True) + NORM_EPS) * g.astype(jnp.float32)


def _l2norm(x):
    return x * lax.rsqrt(jnp.sum(x * x, axis=-1, keepdims=True) + NORM_EPS)


def _flip(t):
    return jnp.flip(t, axis=1)


def _swiglu(h, w_in, w_out):
    gate, up = jnp.split(h @ w_in, 2, axis=-1)
    return (jax.nn.silu(gate) * up) @ w_out


def _to_chunks(t, c):
    b, l, h = t.shape[:3]
    t = t.reshape((b, l // c, c, h) + t.shape[3:])
    return jnp.moveaxis(t, (1, 3), (0, 2))


def _from_chunks(t):
    n, b, h, c, d = t.shape
    return jnp.moveaxis(t, (0, 2), (1, 3)).reshape(b, n * c, h, d)


def _dwconv(x, w):
    pad = DN_CONV // 2
    return lax.conv_general_dilated(x, w[:, None, :], window_strides=(1,), padding=((pad, pad),),
                                    dimension_numbers=('NWC', 'WIO', 'NWC'),
                                    feature_group_count=x.shape[-1])


def _axial_rope(t, rows):
    half = SWA_HD // 2
    quarter = half // 2
    row = jnp.repeat(jnp.arange(rows), GRID_W).astype(jnp.float32)
    col = jnp.tile(jnp.arange(GRID_W), rows).astype(jnp.float32)
    inv = ROPE_BASE ** (-jnp.arange(quarter, dtype=jnp.float32) / quarter)

    def rot(u, pos):
        ang = pos[:, None] * inv[None, :]
        cos = jnp.cos(ang)[None, :, None, :]
        sin = jnp.sin(ang)[None, :, None, :]
        u1, u2 = u[..., :quarter], u[..., quarter:]
        return jnp.concatenate([u1 * cos - u2 * sin, u1 * sin + u2 * cos], axis=-1)

    return jnp.concatenate([rot(t[..., :half], row), rot(t[..., half:], col)], axis=-1)


def _gla_scan(q, k, v, log_a, s0):
    c = GLA_CHUNK
    qc, kc, vc, ac = (_to_chunks(t, c) for t in (q, k, v, log_a))
    b = jnp.cumsum(ac, axis=-2)
    b_last = b[..., -1:, :]
    q_in = qc * jnp.exp(b)
    k_in = kc * jnp.exp(-b)
    k_st = kc * jnp.exp(b_last - b)
    causal = jnp.tril(jnp.ones((c, c), bool))
    attn = jnp.where(causal, jnp.einsum('nbhid,nbhjd->nbhij', q_in, k_in), 0.0)
    o_intra = jnp.einsum('nbhij,nbhjv->nbhiv', attn, vc)

    def step(s, xs):
        q_i, k_i, v_i, dec_i = xs
        o_i = jnp.einsum('bhcd,bhdv->bhcv', q_i, s)
        s = s * dec_i[..., None] + jnp.einsum('bhcd,bhcv->bhdv', k_i, v_i)
        return s, o_i

    s_fin, o_inter = lax.scan(step, s0, (q_in, k_st, vc, jnp.exp(b_last[..., 0, :])))
    return _from_chunks(o_intra + o_inter), s_fin


def _delta_scan(q, k, v, g, beta, s0):
    c = DN_CHUNK
    dv = v.shape[-1]
    qc, kc, vc, gch, bc = (_to_chunks(t, c) for t in (q, k, v, g, beta))
    gcum = jnp.cumsum(gch, axis=-1)
    incl = jnp.tril(jnp.ones((c, c), bool))
    strict = jnp.tril(jnp.ones((c, c), bool), -1)
    diff = gcum[..., :, None] - gcum[..., None, :]
    decay = jnp.where(incl, jnp.exp(jnp.where(incl, diff, 0.0)), 0.0)
    kb = kc * bc[..., None]
    a_mat = jnp.where(strict, jnp.einsum('nbhid,nbhjd->nbhij', kb, kc) * decay, 0.0)
    lhs = a_mat + jnp.eye(c, dtype=a_mat.dtype)
    rhs = jnp.concatenate([vc * bc[..., None], kb * jnp.exp(gcum)[..., None]], axis=-1)
    sol = lax.linalg.triangular_solve(lhs, rhs, left_side=True, lower=True)
    u, w = sol[..., :dv], sol[..., dv:]
    attn = jnp.einsum('nbhid,nbhjd->nbhij', qc, kc) * decay
    q_dec = qc * jnp.exp(gcum)[..., None]
    k_st = kc * jnp.exp(gcum[..., -1:] - gcum)[..., None]

    def step(s, xs):
        q_i, k_i, u_i, w_i, a_i, dec_i = xs
        v_new = u_i - jnp.einsum('bhcd,bhdv->bhcv', w_i, s)
        o_i = jnp.einsum('bhcd,bhdv->bhcv', q_i, s) + jnp.einsum('bhij,bhjv->bhiv', a_i, v_new)
        s = s * dec_i[..., None, None] + jnp.einsum('bhcd,bhcv->bhdv', k_i, v_new)
        return s, o_i

    s_fin, o = lax.scan(step, s0, (q_dec, k_st, u, w, attn, jnp.exp(gcum[..., -1])))
    return _from_chunks(o), s_fin


def _sink_softmax_attend(q, segments, sink):
    scores = []
    for k, _, valid in segments:
        s = jnp.einsum('bqhgd,bkhd->bhgqk', q, k)
        if valid is not None:
            s = jnp.where(valid, s, NEG_INF)
        scores.append(s)
    sk = sink.reshape(1, SWA_KV_HEADS, SWA_GROUP, 1, 1)
    m = sk
    for s in scores:
        m = jnp.maximum(m, jnp.max(s, axis=-1, keepdims=True))
    denom = jnp.exp(sk - m)
    acc = None
    for s, (_, v, _) in zip(scores, segments):
        p = jnp.exp(s - m)
        denom = denom + jnp.sum(p, axis=-1, keepdims=True)
        pv = jnp.einsum('bhgqk,bkhd->bhgqd', p, v)
        acc = pv if acc is None else acc + pv
    return jnp.einsum('bhgqd->bqhgd', acc / denom)


def _swa_context(q, k, v, sink):
    b, l = q.shape[:2]
    nb = l // SWA_BLOCK
    qb = jnp.moveaxis(q.reshape(b, nb, SWA_BLOCK, SWA_KV_HEADS, SWA_GROUP, SWA_HD), 1, 0)
    o = lax.map(lambda qn: _sink_softmax_attend(qn, [(k, v, None)], sink), qb)
    return jnp.moveaxis(o, 0, 1).reshape(b, l, SWA_Q_HEADS * SWA_HD)


def _swa_latent(q, k, v, k_ctx, v_ctx, sink):
    b, l = q.shape[:2]
    nb = l // SWA_BLOCK
    blk = SWA_BLOCK
    qb = jnp.moveaxis(q.reshape(b, nb, blk, SWA_KV_HEADS, SWA_GROUP, SWA_HD), 1, 0)

    def windows(t):
        tp = jnp.pad(t, ((0, 0), (blk, blk), (0, 0), (0, 0))).reshape(b, nb + 2, blk, SWA_KV_HEADS, SWA_HD)
        wdw = jnp.concatenate([tp[:, :-2], tp[:, 1:-1], tp[:, 2:]], axis=2)
        return jnp.moveaxis(wdw, 1, 0)

    kw, vw = windows(k), windows(v)
    qpos = jnp.arange(l).reshape(nb, blk)
    kpos = jnp.arange(nb)[:, None] * blk - blk + jnp.arange(3 * blk)[None, :]
    valid = ((jnp.abs(qpos[:, :, None] - kpos[:, None, :]) <= SWA_WINDOW)
             & (kpos[:, None, :] >= 0) & (kpos[:, None, :] < l))

    def block(args):
        qn, kn, vn, mn = args
        return _sink_softmax_attend(qn, [(kn, vn, mn), (k_ctx, v_ctx, None)], sink)

    o = lax.map(block, (qb, kw, vw, valid))
    return jnp.moveaxis(o, 0, 1).reshape(b, l, SWA_Q_HEADS * SWA_HD)


def _token_mix(h, ctx, w_in, gla_w2, gla_b, gla_ng, dn_conv, dn_a_log, dn_dt_bias, dn_ng, swa_sink, w_branch, w_o):
    f32 = jnp.float32
    b, l, _ = h.shape
    offs = [int(o) for o in np.cumsum(IN_SPLITS)[:-1]]
    parts = [p.astype(f32) for p in jnp.split(h @ w_in, offs, axis=-1)]
    (g_q, g_k, g_v, g_r, g_lr_f, g_lr_b,
     d_q, d_k, d_v, d_a_f, d_a_b, d_b_f, d_b_b, d_g,
     s_q, s_k, s_v, m_g) = parts
    if ctx is None:
        s_gla_f = s_gla_b = jnp.zeros((b, GLA_HEADS, GLA_DK, GLA_DV), f32)
        s_dn_f = s_dn_b = jnp.zeros((b, DN_HEADS, DN_DK, DN_DV), f32)
    else:
        k_ctx, v_ctx, s_gla_f, s_gla_b, s_dn_f, s_dn_b = (t.astype(f32) for t in ctx)

    q = g_q.reshape(b, l, GLA_HEADS, GLA_DK) * GLA_DK ** -0.5
    k = g_k.reshape(b, l, GLA_HEADS, GLA_DK)
    v = g_v.reshape(b, l, GLA_HEADS, GLA_DV)

    def gla_log_decay(lr, d):
        z = lr @ gla_w2[d].astype(f32) + gla_b[d].astype(f32)
        return (jax.nn.log_sigmoid(z) / GLA_TAU).reshape(b, l, GLA_HEADS, GLA_DK)

    o_f, s_gla_f = _gla_scan(q, k, v, gla_log_decay(g_lr_f, 0), s_gla_f)
    o_b, s_gla_b = _gla_scan(_flip(q), _flip(k), _flip(v), _flip(gla_log_decay(g_lr_b, 1)), s_gla_b)
    o_gla = _head_rms(o_f + _flip(o_b), gla_ng) * jax.nn.silu(g_r).reshape(b, l, GLA_HEADS, GLA_DV)

    qkv = jax.nn.silu(_dwconv(jnp.concatenate([d_q, d_k, d_v], axis=-1), dn_conv.astype(f32)))
    nqk = DN_HEADS * DN_DK
    qd = _l2norm(qkv[..., :nqk].reshape(b, l, DN_HEADS, DN_DK)) * DN_DK ** -0.5
    kd = _l2norm(qkv[..., nqk:2 * nqk].reshape(b, l, DN_HEADS, DN_DK))
    vd = qkv[..., 2 * nqk:].reshape(b, l, DN_HEADS, DN_DV)

    def dn_gates(a, bt, d):
        g = -jnp.exp(dn_a_log[d].astype(f32)) * jax.nn.softplus(a + dn_dt_bias[d].astype(f32))
        return g, jax.nn.sigmoid(bt)

    g_f, beta_f = dn_gates(d_a_f, d_b_f, 0)
    g_b, beta_b = dn_gates(d_a_b, d_b_b, 1)
    od_f, s_dn_f = _delta_scan(qd, kd, vd, g_f, beta_f, s_dn_f)
    od_b, s_dn_b = _delta_scan(_flip(qd), _flip(kd), _flip(vd), _flip(g_b), _flip(beta_b), s_dn_b)
    o_dn = _head_rms(od_f + _flip(od_b), dn_ng) * jax.nn.silu(d_g).reshape(b, l, DN_HEADS, DN_DV)

    qs = s_q.reshape(b, l, SWA_Q_HEADS, SWA_HD)
    ks = s_k.reshape(b, l, SWA_KV_HEADS, SWA_HD)
    vs = s_v.reshape(b, l, SWA_KV_HEADS, SWA_HD)
    sink = swa_sink.astype(f32)
    if ctx is None:
        o_swa = _swa_context(qs * SWA_HD ** -0.5, ks, vs, sink)
        new_ctx = (ks, vs, s_gla_f, s_gla_b, s_dn_f, s_dn_b)
    else:
        rows = l // GRID_W
        o_swa = _swa_latent(_axial_rope(qs, rows) * SWA_HD ** -0.5, _axial_rope(ks, rows), vs, k_ctx, v_ctx, sink)
        new_ctx = None

    branches = jnp.stack([o_gla.reshape(b, l, BRANCH_W), o_dn.reshape(b, l, BRANCH_W), o_swa], axis=2).astype(h.dtype)
    u = jnp.einsum('blnw,nwd->blnd', branches, w_branch)
    gates = jax.nn.sigmoid(m_g).reshape(b, l, N_BRANCH, D_MODEL).astype(h.dtype)
    out = jnp.sum(gates * u, axis=2) @ w_o
    return out, new_ctx


def _layer(x, mod, ctx, norm_g, ffn_in, ffn_out, mix_params):
    sh1, sc1, gt1, sh2, sc2, gt2, sh3, sc3, gt3 = (m[:, None, :].astype(x.dtype) for m in jnp.split(mod, N_MOD, axis=-1))
    h = _rms(x, norm_g[0]) * (1 + sc1) + sh1
    x = x + 0.5 * gt1 * _swiglu(h, ffn_in[0], ffn_out[0])
    h = _rms(x, norm_g[1]) * (1 + sc2) + sh2
    mix, new_ctx = _token_mix(h, ctx, *mix_params)
    x = x + gt2 * mix
    h = _rms(x, norm_g[2]) * (1 + sc3) + sh3
    x = x + 0.5 * gt3 * _swiglu(h, ffn_in[1], ffn_out[1])
    return x, new_ctx


def setup_inputs(seed: int = 0) -> dict:
    key = jax.random.key(seed)
    ks = jax.random.split(key, 28)
    f32 = jnp.float32

    def nrm(i, shape, scale):
        return jax.random.normal(ks[i], shape, f32) * scale

    dt = jnp.exp(jax.random.uniform(ks[17], (DEPTH, 2, DN_HEADS), f32, minval=math.log(1e-3), maxval=math.log(1e-1)))
    return {
        'x_prompt': nrm(0, (BATCH, SEQ, D_MODEL), 1.0),
        'x_sample': nrm(1, (DEC_BATCH, DEC_SEQ, D_MODEL), 1.0),
        'cache_attn_k': nrm(2, (DEC_BATCH, DEPTH, PAST_LEN, SWA_KV_HEADS, SWA_HD), 1.0),
        'cache_attn_v': nrm(3, (DEC_BATCH, DEPTH, PAST_LEN, SWA_KV_HEADS, SWA_HD), 1.0),
        'state_gla_fwd': nrm(4, (DEC_BATCH, DEPTH, GLA_HEADS, GLA_DK, GLA_DV), 1.0),
        'state_gla_bwd': nrm(5, (DEC_BATCH, DEPTH, GLA_HEADS, GLA_DK, GLA_DV), 1.0),
        'state_dn_fwd': nrm(6, (DEC_BATCH, DEPTH, DN_HEADS, DN_DK, DN_DV), 0.3),
        'state_dn_bwd': nrm(7, (DEC_BATCH, DEPTH, DN_HEADS, DN_DK, DN_DV), 0.3),
        'c': nrm(8, (DEC_BATCH, D_MODEL), 1.0),
        'c_ctx': nrm(9, (D_MODEL,), 1.0),
        'w_mod': nrm(10, (DEPTH, D_MODEL, N_MOD * D_MODEL), 0.5 * D_MODEL ** -0.5),
        'b_mod': nrm(11, (DEPTH, N_MOD * D_MODEL), 0.01),
        'norm_g': 1.0 + nrm(12, (DEPTH, 3, D_MODEL), 0.02),
        'ffn_in': nrm(13, (DEPTH, 2, D_MODEL, 2 * D_FF), D_MODEL ** -0.5),
        'ffn_out': nrm(14, (DEPTH, 2, D_FF, D_MODEL), D_FF ** -0.5),
        'w_in': nrm(15, (DEPTH, D_MODEL, D_IN), D_MODEL ** -0.5),
        'gla_w2': nrm(16, (DEPTH, 2, GLA_RANK, GLA_HEADS * GLA_DK), GLA_RANK ** -0.5),
        'gla_b': nrm(18, (DEPTH, 2, GLA_HEADS * GLA_DK), 0.1),
        'gla_norm_g': 1.0 + nrm(19, (DEPTH, GLA_DV), 0.02),
        'dn_conv': nrm(20, (DEPTH, DN_CONV, DN_HEADS * (2 * DN_DK + DN_DV)), DN_CONV ** -0.5),
        'dn_a_log': jnp.log(jax.random.uniform(ks[21], (DEPTH, 2, DN_HEADS), f32, minval=1.0, maxval=16.0)),
        'dn_dt_bias': dt + jnp.log(-jnp.expm1(-dt)),
        'dn_norm_g': 1.0 + nrm(22, (DEPTH, DN_DV), 0.02),
        'swa_sink': nrm(23, (DEPTH, SWA_Q_HEADS), 1.0),
        'w_branch': nrm(24, (DEPTH, N_BRANCH, BRANCH_W, D_MODEL), BRANCH_W ** -0.5),
        'w_o': nrm(25, (DEPTH, D_MODEL, D_MODEL), D_MODEL ** -0.5),
        'final_norm_g': 1.0 + nrm(26, (D_MODEL,), 0.02),
    }


def reference(x_prompt, x_sample, cache_attn_k, cache_attn_v, state_gla_fwd, state_gla_bwd, state_dn_fwd,
              state_dn_bwd, c, c_ctx, w_mod, b_mod, norm_g, ffn_in, ffn_out, w_in, gla_w2, gla_b, gla_norm_g,
              dn_conv, dn_a_log, dn_dt_bias, dn_norm_g, swa_sink, w_branch, w_o, final_norm_g):
    xp, xs = x_prompt, x_sample
    cache_in = (cache_attn_k, cache_attn_v, state_gla_fwd, state_gla_bwd, state_dn_fwd, state_dn_bwd)
    new_layers = []
    for l in range(DEPTH):
        mix_params = (w_in[l], gla_w2[l], gla_b[l], gla_norm_g[l], dn_conv[l], dn_a_log[l], dn_dt_bias[l],
                      dn_norm_g[l], swa_sink[l], w_branch[l], w_o[l])
        mod_ctx = jax.nn.silu(c_ctx)[None, :] @ w_mod[l] + b_mod[l]
        mod_lat = jax.nn.silu(c) @ w_mod[l] + b_mod[l]
        xp, ctx_l = _layer(xp, mod_ctx, None, norm_g[l], ffn_in[l], ffn_out[l], mix_params)
        new_layers.append(ctx_l)
        cached_l = tuple(t[:, l] for t in cache_in)
        xs, _ = _layer(xs, mod_lat, cached_l, norm_g[l], ffn_in[l], ffn_out[l], mix_params)
    new_attn_k, new_attn_v, new_gla_fwd, new_gla_bwd, new_dn_fwd, new_dn_bwd = (
        jnp.stack([lay[i] for lay in new_layers], axis=1).astype(x_prompt.dtype) for i in range(6))
    y_prompt = _rms(xp, final_norm_g)
    y_sample = _rms(xs, final_norm_g)
    return (y_prompt, y_sample, new_attn_k, new_attn_v, new_gla_fwd, new_gla_bwd, new_dn_fwd, new_dn_bwd)
```

```python
import numpy as np
from contextlib import ExitStack
import concourse.bass as bass
import concourse.mybir as mybir
from concourse.bass_utils import run_bass_kernel_spmd

F32 = mybir.dt.float32
BF16 = mybir.dt.bfloat16
AF = mybir.ActivationFunctionType
ALU = mybir.AluOpType

D = 2048
KC = 16
DFF = 5504
HC = 43
NL = 2
TP = 1024
TS = 1024
TT = TP + TS
LS = 4096
EPS = 1e-6
DIN = 14912
MG0 = 8768


def mixer_cols(nG, nD, nQ, nKV):
    names = [("g_q", nG * 128), ("g_k", nG * 128), ("g_v", nG * 256), ("g_r", nG * 256), ("g_lf", 16), ("g_lb", 16),
             ("d_q", nD * 128), ("d_k", nD * 128), ("d_v", nD * 128), ("d_af", nD), ("d_ab", nD), ("d_bf", nD),
             ("d_bb", nD), ("d_g", nD * 128), ("s_q", nQ * 128), ("s_k", nKV * 128), ("s_v", nKV * 128)]
    off = {}
    o = 0
    for n, w in names:
        off[n] = o
        o += w
    off["_tot"] = o
    return off


class SemG:
    def __init__(self, sem):
        self.sem = sem
        self.cnt = 0


class Buf:
    def __init__(self, name, semg=None):
        self.name = name
        self.w = None
        self.r = {}
        self.semg = semg


class T:
    def __init__(self, t, buf, th=None):
        self.t = t
        self.buf = buf
        self.th = th

    def __getitem__(self, key):
        return self.t[key]


class Eng:
    def __init__(self, raw, sem, pe=False):
        self.raw = raw
        self.sem = sem
        self.cnt = 0
        self.seen = {}
        self.pe = pe


class KB:
    def __init__(self, nc, es):
        self.nc = nc
        self.es = es
        mk = lambda n: es.enter_context(nc.semaphore(n))
        self.PE = Eng(nc.tensor, mk("s_pe"), pe=True)
        self.ACT = Eng(nc.scalar, mk("s_act"))
        self.DVE = Eng(nc.vector, mk("s_dve"))
        self.POOL = Eng(nc.gpsimd, mk("s_pool"))
        self.SP = Eng(nc.sync, mk("s_sp"))
        self.engs = [self.PE, self.ACT, self.DVE, self.POOL, self.SP]
        self.semgs = []
        self.nsem = 0
        self.ps = []
        self.psi = 0
        self.semg_names = {}

    def semg(self, name=None):
        if name is not None and name in self.semg_names:
            return self.semg_names[name]
        g = SemG(self.es.enter_context(self.nc.semaphore("dq%d" % self.nsem)))
        self.nsem += 1
        self.semgs.append(g)
        if name is not None:
            self.semg_names[name] = g
        return g

    def buf(self, name, dma=False, semg=None):
        return Buf(name, semg if semg is not None else (self.semg(name) if dma else None))

    def sb(self, name, shape, dt, dma=False, semg=None, stack=None):
        self.uid = getattr(self, "uid", 0) + 1
        t = (stack or self.es).enter_context(self.nc.sbuf_tensor("%s_u%d" % (name, self.uid), shape, dt))
        return T(t, self.buf(name, dma, semg))

    def wait(self, E, ev):
        if ev[0] == 'c':
            _, e, v = ev
            if e is E and E.pe:
                return
            if E.seen.get(e, 0) >= v:
                return
            E.raw.wait_ge(e.sem, v)
            E.seen[e] = v
        else:
            g = ev[1]
            v = g.cnt
            if E.seen.get(g, 0) >= v:
                return
            E.raw.wait_ge(g.sem, v)
            E.seen[g] = v

    def op(self, E, fn, R=(), W=(), inc=True):
        for t in R:
            b = t.buf
            if b.w is not None:
                self.wait(E, b.w)
        for t in W:
            b = t.buf
            if b.w is not None:
                self.wait(E, b.w)
            for ev in list(b.r.values()):
                self.wait(E, ev)
        ins = fn()
        if inc:
            E.cnt += 1
            ins.then_inc(E.sem, 1)
            v = E.cnt
        else:
            assert E.pe
            v = E.cnt + 1
        ev = ('c', E, v)
        for t in R:
            t.buf.r[E] = ev
        for t in W:
            t.buf.w = ev
            t.buf.r = {}
        return ins

    def dma(self, Q, out, in_, W, R=(), part=False):
        wb = W.buf
        for t in R:
            if t.buf.w is not None:
                self.wait(Q, t.buf.w)
        if wb.w is not None and not (part and wb.w[0] == 'd' and wb.w[1] is wb.semg):
            self.wait(Q, wb.w)
        for ev in list(wb.r.values()):
            self.wait(Q, ev)
        g = wb.semg
        Q.raw.dma_start(out=out, in_=in_).then_inc(g.sem, 16)
        g.cnt += 16
        ev = ('d', g)
        for t in R:
            t.buf.r[g] = ev
        wb.w = ev
        wb.r = {}

    def barrier(self):
        for E in self.engs:
            for e2 in self.engs:
                if e2 is not E and e2.cnt > 0:
                    self.wait(E, ('c', e2, e2.cnt))
            for g in self.semgs:
                if g.cnt > 0:
                    self.wait(E, ('d', g))

    def mm(self, out, lhsT, rhs, start, stop, R, W, inc=None):
        if inc is None:
            inc = stop
        return self.op(self.PE, lambda: self.nc.tensor.matmul(out, lhsT, rhs, start=start, stop=stop), R, W, inc=inc)

    def act(self, out, in_, func, R, W, bias=None, scale=None):
        kw = {}
        if bias is not None:
            kw["bias"] = bias
        if scale is not None:
            kw["scale"] = scale
        return self.op(self.ACT, lambda: self.nc.scalar.activation(out=out, in_=in_, func=func, **kw), R, W)

    def ve(self, fn, R, W):
        return self.op(self.DVE, fn, R, W)

    def psum(self):
        p = self.ps[self.psi % getattr(self, "ps_limit", len(self.ps))]
        self.psi += 1
        return p


class Prog:
    def __init__(self, nc, es, cfg):
        self.nc = nc
        self.es = es
        self.cfg = cfg
        self.k = KB(nc, es)
        self.dr = {}

    def din(self, name, shape, dt=F32):
        t = self.nc.dram_tensor(name, list(shape), dt, kind="ExternalInput")
        a = T(t.ap(), Buf(name), t)
        self.dr[name] = a
        return a

    def dout(self, name, shape, dt=F32):
        t = self.nc.dram_tensor(name, list(shape), dt, kind="ExternalOutput")
        a = T(t.ap(), self.k.buf(name, dma=True), t)
        self.dr[name] = a
        return a

    def dint(self, name, shape, dt=F32):
        t = self.nc.dram_tensor(name, list(shape), dt)
        a = T(t.ap(), self.k.buf(name, dma=True), t)
        self.dr[name] = a
        return a

    def setup(self):
        k, nc = self.k, self.nc
        for i in range(8):
            p = self.es.enter_context(nc.psum_tensor("ps%d" % i, [128, 512], F32))
            k.ps.append(T(p, Buf("ps%d" % i)))
        cst = self.din("cst", [128, 9, 128])
        self.cst = k.sb("cst_sb", [128, 9, 128], F32, dma=True)
        k.dma(k.SP, self.cst[:], cst[:, :, :], self.cst)
        self.cstb = k.sb("cstb_sb", [128, 9, 128], BF16)
        k.ve(lambda: nc.vector.tensor_copy(out=self.cstb[:], in_=self.cst[:]), [self.cst], [self.cstb])
        bmd = self.din("bmask", [128, 7, 128])
        self.bmask = k.sb("bmask_sb", [128, 7, 128], F32, dma=True)
        k.dma(k.SP, self.bmask[:], bmd[:, :, :], self.bmask)
        self.ones1 = k.sb("ones1", [1, 512], F32)
        k.ve(lambda: nc.vector.memset(self.ones1[:], 1.0), [], [self.ones1])
        ngd = self.din("ngT", [128, 7, 16])
        self.ng = k.sb("ng_sb", [128, 7, 16], F32, dma=True)
        k.dma(k.SP, self.ng[:], ngd[:, :, :], self.ng)
        cvd = self.din("cv", [128, 16, 2])
        cv = k.sb("cv_sb", [128, 16, 2], F32, dma=True)
        k.dma(k.SP, cv[:], cvd[:, :, :], cv)
        self.scv = k.sb("scv", [128, 16, 2], BF16)
        k.act(self.scv[:], cv[:], AF.Silu, [cv], [self.scv])
        bmd = self.din("bmT", [128, NL, 144])
        self.bm = k.sb("bm_sb", [128, NL, 144], F32, dma=True)
        k.dma(k.SP, self.bm[:], bmd[:, :, :], self.bm)
        self.modc = [k.sb("modc%d" % l, [128, 144, 2], F32) for l in range(NL)]
        self.wslot = [k.sb("wslot%d" % i, [128, 16 * 512], BF16, dma=True) for i in range(3)]
        self.wsi = 0
        self.oslot = [k.sb("oslot%d" % i, [128, 43 * 128], BF16, dma=True) for i in range(2)]
        self.osi = 0
        self.xs = self.dint("xs", [16, 128, TT])

    def next_wslot(self):
        s = self.wslot[self.wsi % 3]
        self.wsi += 1
        return s

    def next_oslot(self):
        s = self.oslot[self.osi % 2]
        self.osi += 1
        return s

    def load_w(self, wap, c0, width, slot=None):
        k = self.k
        s = slot or self.next_wslot()
        v = s[:, 0:16 * width].rearrange("p (c n) -> p c n", c=16)
        src = wap[:, c0:c0 + width].rearrange("(c p) n -> p c n", p=128)
        k.dma(k.POOL, v, src, s)
        return s, v

    def mods(self, l):
        k, nc = self.k, self.nc
        wm = self.dr["w_mod"].t
        for g in range(36):
            s, v = self.load_w(wm[l], g * 512, 512)
            ps = k.psum()
            for m in range(4):
                for c in range(16):
                    k.mm(ps[:, 2 * m:2 * m + 2], v[:, c, m * 128:(m + 1) * 128], self.scv[:, c, :], c == 0, c == 15,
                         [s, self.scv], [ps], inc=(c == 15))
            k.ve(lambda: nc.vector.tensor_tensor(
                out=self.modc[l][:, g * 4:(g + 1) * 4, :], in0=ps[:, 0:8].rearrange("p (m t) -> p m t", t=2),
                in1=self.bm[:, l, g * 4:(g + 1) * 4].unsqueeze(2).broadcast_to([128, 4, 2]), op=ALU.add),
                [ps, self.bm], [self.modc[l]])
        mc = self.modc[l]
        for i in range(3):
            sc = mc[:, (3 * i + 1) * 16:(3 * i + 2) * 16, :]
            k.ve(lambda: nc.vector.scalar_tensor_tensor(
                out=sc, in0=sc, scalar=1.0, in1=self.ng[:, l * 3 + i, :].unsqueeze(2).broadcast_to([128, 16, 2]),
                op0=ALU.add, op1=ALU.mult), [mc, self.ng], [mc])
            if i != 1:
                gt = mc[:, (3 * i + 2) * 16:(3 * i + 3) * 16, :]
                k.ve(lambda: nc.vector.tensor_scalar_mul(out=gt, in0=gt, scalar1=0.5), [mc], [mc])

    def norm_block(self, l, i, t0, ntok, which, x_sb, hT, st):
        k, nc = self.k, self.nc
        k.dma(k.SP, x_sb[:, :, 0:ntok], self.xs[:, :, t0:t0 + ntok].rearrange("c p t -> p c t"), x_sb, [self.xs])
        sq = st["sq"]
        ps = k.psum()
        for c in range(16):
            q = sq[c % 2]
            k.act(q[:, 0:ntok], x_sb[:, c, 0:ntok], AF.Square, [x_sb], [q])
            k.mm(ps[:, 0:ntok], self.cst[:, 1, :], q[:, 0:ntok], c == 0, c == 15, [self.cst, q], [ps], inc=True)
        rstd = st["rstd"]
        k.act(rstd[:, 0:ntok], ps[:, 0:ntok], AF.Sqrt, [ps], [rstd], bias=EPS, scale=1.0 / D)
        k.ve(lambda: nc.vector.reciprocal(out=rstd[:, 0:ntok], in_=rstd[:, 0:ntok]), [rstd], [rstd])
        for c in range(16):
            tmp = sq[c % 2]
            if i < 3:
                mc = self.modc[l]
                gcol = mc[:, (3 * i + 1) * 16 + c, which:which + 1]
                shcol = mc[:, (3 * i) * 16 + c, which:which + 1]
                k.ve(lambda: nc.vector.scalar_tensor_tensor(out=tmp[:, 0:ntok], in0=x_sb[:, c, 0:ntok], scalar=gcol,
                                                            in1=rstd[:, 0:ntok], op0=ALU.mult, op1=ALU.mult),
                     [x_sb, mc, rstd], [tmp])
                k.act(hT[:, c, 0:ntok], tmp[:, 0:ntok], AF.Identity, [tmp, mc], [hT], bias=shcol, scale=1.0)
            else:
                gcol = self.ng[:, 6, c:c + 1]
                k.ve(lambda: nc.vector.scalar_tensor_tensor(out=hT[:, c, 0:ntok], in0=x_sb[:, c, 0:ntok], scalar=gcol,
                                                            in1=rstd[:, 0:ntok], op0=ALU.mult, op1=ALU.mult),
                     [x_sb, self.ng, rstd], [hT])

    def ffn(self, l, f):
        k, nc = self.k, self.nc
        i = 0 if f == 0 else 2
        win = self.dr["ffn_in"].t[l, f]
        wout = self.dr["ffn_out"].t[l, f]
        with ExitStack() as st_:
            x_sb = k.sb("fx", [128, 16, 512], F32, dma=True, stack=st_)
            hT = k.sb("fh", [128, 16, 512], BF16, stack=st_)
            actT = k.sb("fa", [128, HC, 512], BF16, stack=st_)
            sg = k.sb("fsg", [128, 512], F32, stack=st_)
            st = {"sq": [k.sb("fsq%d" % j, [128, 512], F32, stack=st_) for j in range(2)],
                  "rstd": k.sb("frs", [128, 512], F32, stack=st_)}
            for blk in range(TT // 512):
                t0 = blk * 512
                which = 0 if t0 < TP else 1
                self.norm_block(l, i, t0, 512, which, x_sb, hT, st)
                ng_ = (DFF + 255) // 256
                for g in range(ng_):
                    h0 = g * 256
                    hw = min(256, DFF - h0)
                    s = self.next_wslot()
                    v = s[:, :].rearrange("p (c n) -> p c n", c=16)
                    k.dma(k.POOL, v[:, :, 0:hw], win[:, h0:h0 + hw].rearrange("(c p) n -> p c n", p=128), s)
                    k.dma(k.POOL, v[:, :, 256:256 + hw], win[:, DFF + h0:DFF + h0 + hw].rearrange("(c p) n -> p c n", p=128),
                          s, part=True)
                    for j in range(hw // 128):
                        pg = k.psum()
                        pu = k.psum()
                        for c in range(16):
                            k.mm(pg[:], v[:, c, j * 128:(j + 1) * 128], hT[:, c, :], c == 0, c == 15, [s, hT], [pg])
                        for c in range(16):
                            k.mm(pu[:], v[:, c, 256 + j * 128:256 + (j + 1) * 128], hT[:, c, :], c == 0, c == 15, [s, hT], [pu])
                        k.act(sg[:], pg[:], AF.Silu, [pg], [sg])
                        hc = (h0 // 128) + j
                        k.ve(lambda: nc.vector.tensor_tensor(out=actT[:, hc, :], in0=sg[:], in1=pu[:], op=ALU.mult),
                             [sg, pu], [actT])
                mc = self.modc[l]
                for m in range(16):
                    s = self.next_oslot()
                    v = s[:, :].rearrange("p (c n) -> p c n", c=HC)
                    k.dma(k.POOL, v, wout[:, m * 128:(m + 1) * 128].rearrange("(c p) n -> p c n", p=128), s)
                    py = k.psum()
                    for c in range(HC):
                        k.mm(py[:], v[:, c, :], actT[:, c, :], c == 0, c == HC - 1, [s, actT], [py])
                    gt = mc[:, (3 * i + 2) * 16 + m, which:which + 1]
                    k.ve(lambda: nc.vector.scalar_tensor_tensor(out=x_sb[:, m, :], in0=py[:], scalar=gt, in1=x_sb[:, m, :],
                                                                op0=ALU.mult, op1=ALU.add), [py, mc, x_sb], [x_sb])
                k.dma(k.SP, self.xs[:, :, t0:t0 + 512].rearrange("c p t -> p c t"), x_sb[:, :, :], self.xs, [x_sb], part=True)
            k.barrier()

    def final(self):
        k, nc = self.k, self.nc
        with ExitStack() as st_:
            x_sb = k.sb("nx", [128, 16, 512], F32, dma=True, stack=st_)
            y_sb = [k.sb("ny%d" % j, [128, 16, 512], F32, stack=st_) for j in range(2)]
            st = {"sq": [k.sb("nsq%d" % j, [128, 512], F32, stack=st_) for j in range(2)],
                  "rstd": k.sb("nrs", [128, 512], F32, stack=st_)}
            yT = self.dr["yT"]
            for blk in range(TT // 512):
                t0 = blk * 512
                y = y_sb[blk % 2]
                self.norm_block(0, 3, t0, 512, 0, x_sb, y, st)
                k.dma(k.SP, yT[:, :, t0:t0 + 512].rearrange("c p t -> p c t"), y[:, :, :], yT, [y], part=True)
            k.barrier()

    def mk_job(self, kind, l):
        J = type("J", (), {})()
        J.kind = kind
        if kind == 'p':
            J.nG, J.nD, J.nQ, J.nKV, J.L = 4, 8, 8, 2, TP
            J.seqs = [(s_ * 256, 2) for s_ in range(4)]
            J.W = self.dr["w_in"].t[l]
        else:
            J.nG, J.nD, J.nQ, J.nKV, J.L = 1, 2, 2, 1, LS
            J.seqs = [(0, 32)]
            J.W = self.dr["w_in_s"].t[l]
        J.cols = mixer_cols(J.nG, J.nD, J.nQ, J.nKV)
        nG, nD, nQ, nKV = J.nG, J.nD, J.nQ, J.nKV
        J.fq = [("g_q", nG * 128), ("g_k", nG * 128), ("g_r", nG * 256), ("g_lf", 16), ("g_lb", 16), ("d_q", nD * 128),
                ("d_k", nD * 128), ("d_v", nD * 128), ("d_g", nD * 128), ("s_q", nQ * 128), ("s_k", nKV * 128)]
        J.pfi = {}
        o = 0
        for n, w in J.fq:
            J.pfi[n] = o
            o += (w + 127) // 128
        J.NF = o
        J.tq = [("g_k", J.cols["g_k"], nG * 128), ("g_v", J.cols["g_v"], nG * 256), ("d_ab", J.cols["d_af"], 4 * nD),
                ("s_v", J.cols["s_v"], nKV * 128), ("s_k", J.cols["s_k"], nKV * 128)]
        J.ptc = {}
        o = 0
        for n, c0, w in J.tq:
            J.ptc[n] = o
            o += w
        J.NT = o
        sfx = "_" + kind
        if ("PF" + sfx) not in self.dr:
            self.dint("PF" + sfx, [J.NF, 128, J.L])
            self.dint("PT" + sfx, [128, J.L // 128, J.NT])
            self.dint("OA" + sfx, [nG * 2 + nD, 128, J.L])
            if kind == 'p':
                self.dint("BR" + sfx, [24, 128, J.L], BF16)
        J.PF, J.PT, J.OA = self.dr["PF" + sfx], self.dr["PT" + sfx], self.dr["OA" + sfx]
        J.BR = self.dr["BR_p"] if kind == 'p' else self.dr["BRS"]
        J.nbr = 8 if kind == 'p' else 2
        return J

    def proj(self, J, hsrc):
        k, nc = self.k, self.nc
        with ExitStack() as st_:
            hb = k.sb("pj_h", [128, 16, 1024], BF16, dma=True, stack=st_)
            stg = [k.sb("pj_s%d" % j, [128, 512], F32, stack=st_) for j in range(3)]
            si = 0
            for tb in range(J.L // 1024):
                for pi, (c0, c1, ap_) in enumerate(hsrc(tb)):
                    k.dma(k.SP, hb[:, c0:c1, :], ap_, hb, [self.HT, self.HG], part=(pi > 0))
                for name, width in J.fq:
                    col0 = J.cols[name]
                    done = 0
                    while done < width:
                        w = min(512, width - done)
                        s, v = self.load_w(J.W, col0 + done, w)
                        for j in range(0, w, 128):
                            rows = min(128, w - j)
                            idx = J.pfi[name] + (done + j) // 128
                            for half in range(2):
                                ps = k.psum()
                                for c in range(16):
                                    k.mm(ps[0:rows, :], v[:, c, j:j + rows], hb[:, c, half * 512:(half + 1) * 512], c == 0, c == 15,
                                         [s, hb], [ps])
                                sg = stg[si % 3]
                                si += 1
                                k.act(sg[0:rows, :], ps[0:rows, :], AF.Copy, [ps], [sg])
                                a = tb * 1024 + half * 512
                                k.dma(k.SP, J.PF[idx, 0:rows, a:a + 512], sg[0:rows, :], J.PF, [sg], part=True)
                        done += w
                for name, col0, width in J.tq:
                    done = 0
                    while done < width:
                        w = min(512, width - done)
                        s, v = self.load_w(J.W, col0 + done, w)
                        for tl in range(8):
                            ps = k.psum()
                            for c in range(16):
                                k.mm(ps[:, 0:w], hb[:, c, tl * 128:(tl + 1) * 128], v[:, c, 0:w], c == 0, c == 15, [s, hb], [ps])
                            sg = stg[si % 3]
                            si += 1
                            k.act(sg[:, 0:w], ps[:, 0:w], AF.Copy, [ps], [sg])
                            c0 = J.ptc[name] + done
                            k.dma(k.SP, J.PT[:, tb * 8 + tl, c0:c0 + w], sg[:, 0:w], J.PT, [sg], part=True)
                        done += w
            k.barrier()

    def gla(self, J, l, st_):
        k, nc = self.k, self.nc
        cst, cstb = self.cst, self.cstb
        NSET = min(3, J.nG)
        sets = []
        for si_ in range(NSET):
            sb = lambda n, sh, dt, dma=False, si_=si_: k.sb("gl%d_" % si_ + n, sh, dt, dma=dma, stack=st_,
                                                            semg=(k.semg("glset%d" % si_) if dma else None))
            lrT = sb("lr", [16, 128], F32, True); qT = sb("q", [128, 128], F32, True); kT = sb("k", [128, 128], F32, True)
            ktok = sb("kt", [128, 128], F32, True); vtok = sb("vt", [128, 256], F32, True)
            la = sb("la", [128, 128], F32); e1 = sb("e1", [128, 128], F32); e2 = sb("e2", [128, 128], F32)
            dcol = sb("dc", [128, 1], F32); tb_ = sb("tb", [128, 128], F32); est = sb("es", [128, 128], F32)
            qin = sb("qi", [128, 128], BF16); kin = sb("ki", [128, 128], BF16); kst = sb("ks", [128, 128], BF16)
            vb = sb("vb", [128, 256], BF16); at = sb("at", [128, 128], BF16)
            ost = sb("os", [128, 2, 128], F32); opv = sb("op", [128, 2, 128], F32, True); osq = sb("oq", [128, 2, 128], F32)
            rstd = sb("rs", [128, 128], F32); rT = sb("r", [128, 2, 128], F32, True); t1 = sb("t1", [128, 128], F32)
            brs = sb("br", [128, 2, 128], BF16)
            sets.append((lrT, qT, kT, ktok, vtok, la, e1, e2, dcol, tb_, est, qin, kin, kst, vb, at, ost, opv, osq, rstd, rT, t1, brs))
        sb = lambda n, sh, dt, dma=False: k.sb("gl_" + n, sh, dt, dma=dma, stack=st_, semg=(k.semg("glSS") if dma else None))
        S_l = [sb("S%d" % h_, [128, 256], F32, True) for h_ in range(J.nG)]
        Sb_l = [sb("Sb%d" % h_, [128, 256], BF16) for h_ in range(J.nG)]
        PF, PT, pfi, ptc = J.PF, J.PT, J.pfi, J.ptc
        w2, gb, gng = J.gw2, J.gb, self.gng

        def step(TS, S, Sb, s0, nts, t, h, d):
            (lrT, qT, kT, ktok, vtok, la, e1, e2, dcol, tb_, est, qin, kin, kst, vb, at, ost, opv, osq, rstd, rT, t1, brs) = TS
            INC = 2 if d == 0 else 3
            a = s0 + t * 128
            ti = a // 128
            k.dma(k.SP, lrT[:], PF[pfi["g_lf"] + d, 0:16, a:a + 128], lrT, [PF])
            k.dma(k.SP, qT[:], PF[pfi["g_q"] + h, :, a:a + 128], qT, [PF])
            k.dma(k.SP, kT[:], PF[pfi["g_k"] + h, :, a:a + 128], kT, [PF])
            k.dma(k.SP, ktok[:], PT[:, ti, ptc["g_k"] + h * 128:ptc["g_k"] + (h + 1) * 128], ktok, [PT])
            k.dma(k.SP, vtok[:], PT[:, ti, ptc["g_v"] + h * 256:ptc["g_v"] + (h + 1) * 256], vtok, [PT])
            pz = k.psum()
            k.mm(pz[:, 0:128], lrT[:, :], w2[0:16, d, h * 128:(h + 1) * 128], True, False, [lrT, w2], [pz], inc=False)
            k.mm(pz[:, 0:128], self.ones1[0:1, 0:128], gb[0:1, d, h * 128:(h + 1) * 128], False, True, [self.ones1, gb], [pz])
            k.act(la[:], pz[:, 0:128], AF.Exp, [pz], [la], scale=-1.0)
            k.act(la[:], la[:], AF.Ln, [la], [la], bias=1.0)
            pc = k.psum()
            k.mm(pc[:, 0:128], la[:], cst[:, INC, :], True, True, [la, cst], [pc], inc=False)
            k.mm(pc[:, 128:256], la[:], cst[:, 1, :], True, True, [la, cst], [pc], inc=False)
            k.mm(pc[:, 256:384], cst[:, INC, :], la[:], True, True, [la, cst], [pc], inc=False)
            k.mm(pc[:, 384:512], cst[:, 1, :], la[:], True, True, [la, cst], [pc])
            k.act(e1[:], pc[:, 0:128], AF.Exp, [pc], [e1], scale=-1.0 / 16)
            k.act(e2[:], pc[:, 0:128], AF.Exp, [pc], [e2], scale=1.0 / 16)
            k.act(dcol[:], pc[:, 128:129], AF.Exp, [pc], [dcol], scale=-1.0 / 16)
            k.act(tb_[:], pc[:, 256:384], AF.Copy, [pc], [tb_])
            k.ve(lambda: nc.vector.tensor_tensor(out=tb_[:], in0=pc[:, 384:512], in1=tb_[:], op=ALU.subtract), [pc, tb_], [tb_])
            k.act(est[:], tb_[:], AF.Exp, [tb_], [est], scale=-1.0 / 16)
            k.ve(lambda: nc.vector.scalar_tensor_tensor(out=qin[:], in0=qT[:], scalar=128.0 ** -0.5, in1=e1[:], op0=ALU.mult,
                                                        op1=ALU.mult), [qT, e1], [qin])
            k.ve(lambda: nc.vector.tensor_tensor(out=kin[:], in0=kT[:], in1=e2[:], op=ALU.mult), [kT, e2], [kin])
            k.ve(lambda: nc.vector.tensor_tensor(out=kst[:], in0=ktok[:], in1=est[:], op=ALU.mult), [ktok, est], [kst])
            k.act(vb[:], vtok[:], AF.Copy, [vtok], [vb])
            pa = k.psum()
            k.mm(pa[:, 0:128], kin[:], qin[:], True, True, [kin, qin], [pa])
            k.ve(lambda: nc.vector.tensor_tensor(out=at[:], in0=pa[:, 0:128], in1=cst[:, INC, :], op=ALU.mult), [pa, cst], [at])
            po = k.psum()
            for c in range(2):
                k.mm(po[:, c * 128:(c + 1) * 128], vb[:, c * 128:(c + 1) * 128], at[:], True, False, [vb, at], [po], inc=False)
                k.mm(po[:, c * 128:(c + 1) * 128], Sb[:, c * 128:(c + 1) * 128], qin[:], False, True, [Sb, qin], [po], inc=(c == 1))
            oa = J.OA[2 * h:2 * h + 2, :, a:a + 128].rearrange("c p t -> p c t")
            pov = po[:, 0:256].rearrange("p (c t) -> p c t", c=2)
            if d == 0:
                k.act(ost[:], pov, AF.Copy, [po], [ost])
                k.dma(k.SP, oa, ost[:], J.OA, [ost], part=True)
            else:
                k.dma(k.SP, opv[:], oa, opv, [J.OA])
                k.ve(lambda: nc.vector.tensor_tensor(out=ost[:], in0=pov, in1=opv[:], op=ALU.add), [po, opv], [ost])
                k.act(osq[:], ost[:], AF.Square, [ost], [osq])
                pr = k.psum()
                k.mm(pr[:, 0:128], cst[:, 1, :], osq[:, 0, :], True, False, [cst, osq], [pr], inc=False)
                k.mm(pr[:, 0:128], cst[:, 1, :], osq[:, 1, :], False, True, [cst, osq], [pr])
                k.act(rstd[:], pr[:, 0:128], AF.Sqrt, [pr], [rstd], bias=EPS, scale=1.0 / 256)
                k.ve(lambda: nc.vector.reciprocal(out=rstd[:], in_=rstd[:]), [rstd], [rstd])
                k.dma(k.SP, rT[:], PF[pfi["g_r"] + 2 * h:pfi["g_r"] + 2 * h + 2, :, a:a + 128].rearrange("c p t -> p c t"), rT, [PF])
                k.act(rT[:], rT[:], AF.Silu, [rT], [rT])
                for c in range(2):
                    k.ve(lambda: nc.vector.scalar_tensor_tensor(out=t1[:], in0=ost[:, c, :], scalar=gng[:, l, c:c + 1], in1=rstd[:],
                                                                op0=ALU.mult, op1=ALU.mult), [ost, gng, rstd], [t1])
                    k.ve(lambda: nc.vector.tensor_tensor(out=brs[:, c, :], in0=t1[:], in1=rT[:, c, :], op=ALU.mult), [t1, rT], [brs])
                k.dma(k.SP, J.BR[2 * h:2 * h + 2, :, a:a + 128].rearrange("c p t -> p c t"), brs[:], J.BR, [brs], part=True)
            pS = k.psum()
            k.mm(pS[:, 0:256], kst[:], vb[:], True, True, [kst, vb], [pS])
            k.ve(lambda: nc.vector.scalar_tensor_tensor(out=S[:], in0=S[:], scalar=dcol[:, 0:1], in1=pS[:, 0:256], op0=ALU.mult,
                                                        op1=ALU.add), [S, dcol, pS], [S])
            k.act(Sb[:], S[:], AF.Copy, [S], [Sb])

        for (s0, nts) in J.seqs:
            for d in range(2):
                for h in range(J.nG):
                    if J.kind == 'p':
                        k.ve(lambda: nc.vector.memset(S_l[h][:], 0.0), [], [S_l[h]])
                    else:
                        k.dma(k.SP, S_l[h][:], self.dr["sg_in"].t[l, d], S_l[h])
                    k.act(Sb_l[h][:], S_l[h][:], AF.Copy, [S_l[h]], [Sb_l[h]])
                order = list(range(nts)) if d == 0 else list(range(nts - 1, -1, -1))
                for t in order:
                    for h in range(J.nG):
                        step(sets[h % NSET], S_l[h], Sb_l[h], s0, nts, t, h, d)
                if J.kind == 'p':
                    so = self.dr["o_gf" if d == 0 else "o_gb"]
                    for h in range(J.nG):
                        k.dma(k.SP, so[l, s0 // 256, h], S_l[h][:], so, [S_l[h]], part=True)

    def dn(self, J, l, st_):
        k, nc = self.k, self.nc
        cst, cstb = self.cst, self.cstb
        NSET = min(3, J.nD)
        sets = []
        for si_ in range(NSET):
            sb = lambda n, sh, dt, dma=False, si_=si_: k.sb("dn%d_" % si_ + n, sh, dt, dma=dma, stack=st_,
                                                            semg=(k.semg("dnset%d" % si_) if dma else None))
            xq = sb("xq", [128, 132], F32, True); xk = sb("xk", [128, 132], F32, True); xv = sb("xv", [128, 132], F32, True)
            cq = sb("cq", [128, 128], F32); ck = sb("ck", [128, 128], F32); cv_ = sb("cv", [128, 128], F32)
            sq2 = sb("sq2", [128, 256], F32); rn = sb("rn", [128, 256], F32)
            qn = sb("qn", [128, 128], F32); kn = sb("kn", [128, 128], F32); qnb = sb("qnb", [128, 128], BF16); knb = sb("knb", [128, 128], BF16)
            ktv = sb("ktv", [128, 256], F32)
            ab = sb("ab", [128, 4 * J.nD], F32, True)
            ea = sb("ea", [128, 1], F32); gcl = sb("g", [128, 1], F32); beta = sb("be", [128, 1], F32); nbeta = sb("nbe", [128, 1], F32)
            Dg = sb("Dg", [128, 128], F32); Db = sb("Db", [128, 128], F32)
            gcs = sb("gcs", [128, 130], F32); brow = sb("brow", [128, 128], F32); ngc = sb("ngc", [128, 1], F32)
            tm1 = sb("tm1", [128, 128], F32); DTi = sb("DTi", [128, 128], F32); tm2 = sb("tm2", [128, 128], F32); Di = sb("Di", [128, 128], F32)
            YX = sb("YX", [128, 256], F32)
            EXF = sb("EXF", [128, 2, 7, 128], F32)
            DR = [sb("DR%d" % j, [128, 256], F32) for j in range(2)]
            WZ = sb("WZ", [128, 256], F32)
            Rb = sb("Rb", [128, 128], BF16)
            attn = sb("attn", [128, 128], BF16)
            eg = sb("eg", [128, 1], F32); bg = sb("bg", [128, 1], F32); kstc = sb("kstc", [128, 1], F32); egl = sb("egl", [128, 1], F32)
            vbt = sb("vbt", [128, 128], BF16); kbe = sb("kbe", [128, 128], BF16); kst = sb("kst", [128, 128], BF16)
            erow = sb("erow", [128, 128], F32); qdec = sb("qdec", [128, 128], BF16)
            nwT = sb("nwT", [128, 128], BF16); vnew = sb("vnew", [128, 128], BF16)
            ost = sb("os", [128, 128], F32); opv = sb("op", [128, 128], F32, True); osq = sb("oq", [128, 128], F32)
            rstd = sb("rs", [128, 128], F32); gT = sb("gT", [128, 128], F32, True); t1 = sb("t1", [128, 128], F32); brs = sb("br", [128, 128], BF16)
            sets.append((xq, xk, xv, cq, ck, cv_, sq2, rn, qn, kn, qnb, knb, ktv, ab, ea, gcl, beta, nbeta, Dg, Db, gcs, brow, ngc, tm1, DTi, tm2, Di, YX, EXF, DR, WZ, Rb, attn, eg, bg, kstc, egl, vbt, kbe, kst, erow, qdec, nwT, vnew, ost, opv, osq, rstd, gT, t1, brs))
        sb = lambda n, sh, dt, dma=False: k.sb("dn_" + n, sh, dt, dma=dma, stack=st_, semg=(k.semg("dnSS") if dma else None))
        S_l = [sb("S%d" % h_, [128, 128], F32, True) for h_ in range(J.nD)]
        Sb_l = [sb("Sb%d" % h_, [128, 128], BF16) for h_ in range(J.nD)]
        PF, PT, pfi, ptc, nD = J.PF, J.PT, J.pfi, J.ptc, J.nD
        cw, dtb, nA, dng = J.cw, J.dtb, J.nA, self.dng

        def step(TS, S, Sb, s0, nts, t, h, d):
            (xq, xk, xv, cq, ck, cv_, sq2, rn, qn, kn, qnb, knb, ktv, ab, ea, gcl, beta, nbeta, Dg, Db, gcs, brow, ngc, tm1, DTi, tm2, Di, YX, EXF, DR, WZ, Rb, attn, eg, bg, kstc, egl, vbt, kbe, kst, erow, qdec, nwT, vnew, ost, opv, osq, rstd, gT, t1, brs) = TS
            INC, STR, NEGI = (2, 4, 6) if d == 0 else (3, 5, 7)
            STRT, NEGIT = (5, 7) if d == 0 else (4, 6)
            a = s0 + t * 128
            ti = a // 128
            lo = max(s0, a - 2)
            hi = min(s0 + nts * 128, a + 130)
            for (x_, nm, blk) in ((xq, "d_q", h), (xk, "d_k", nD + h), (xv, "d_v", 2 * nD + h)):
                if lo > a - 2 or hi < a + 130:
                    k.ve(lambda: nc.vector.memset(x_[:], 0.0), [], [x_])
                k.dma(k.SP, x_[:, lo - (a - 2):hi - (a - 2)], PF[pfi[nm] + h, :, lo:hi], x_, [PF], part=True)
            for (x_, c_, blk) in ((xq, cq, h), (xk, ck, nD + h), (xv, cv_, 2 * nD + h)):
                k.ve(lambda: nc.vector.tensor_scalar_mul(out=c_[:], in0=x_[:, 0:128], scalar1=cw[:, blk, 0:1]), [x_, cw], [c_])
                for tp in range(1, 5):
                    k.ve(lambda: nc.vector.scalar_tensor_tensor(out=c_[:], in0=x_[:, tp:tp + 128], scalar=cw[:, blk, tp:tp + 1], in1=c_[:],
                                                                op0=ALU.mult, op1=ALU.add), [x_, cw, c_], [c_])
                k.act(c_[:], c_[:], AF.Silu, [c_], [c_])
            k.act(sq2[:, 0:128], cq[:], AF.Square, [cq], [sq2])
            k.act(sq2[:, 128:256], ck[:], AF.Square, [ck], [sq2])
            pn = k.psum()
            k.mm(pn[:, 0:256], cst[:, 1, :], sq2[:], True, True, [cst, sq2], [pn])
            k.act(rn[:], pn[:, 0:256], AF.Sqrt, [pn], [rn], bias=EPS, scale=1.0)
            k.ve(lambda: nc.vector.reciprocal(out=rn[:], in_=rn[:]), [rn], [rn])
            k.ve(lambda: nc.vector.scalar_tensor_tensor(out=qn[:], in0=cq[:], scalar=128.0 ** -0.5, in1=rn[:, 0:128], op0=ALU.mult,
                                                        op1=ALU.mult), [cq, rn], [qn])
            k.ve(lambda: nc.vector.tensor_tensor(out=kn[:], in0=ck[:], in1=rn[:, 128:256], op=ALU.mult), [ck, rn], [kn])
            k.act(qnb[:], qn[:], AF.Copy, [qn], [qnb])
            k.act(knb[:], kn[:], AF.Copy, [kn], [knb])
            pt_ = k.psum()
            k.op(k.PE, lambda: nc.tensor.transpose(pt_[:, 0:128], kn[:], cst[:, 0, :]), [kn, cst], [pt_], inc=False)
            k.op(k.PE, lambda: nc.tensor.transpose(pt_[:, 128:256], cv_[:], cst[:, 0, :]), [cv_, cst], [pt_])
            k.act(ktv[:], pt_[:, 0:256], AF.Copy, [pt_], [ktv])
            k.dma(k.SP, ab[:], PT[:, ti, ptc["d_ab"]:ptc["d_ab"] + 4 * nD], ab, [PT])
            k.act(ea[:], ab[:, d * nD + h:d * nD + h + 1], AF.Exp, [ab, dtb], [ea], bias=dtb[:, d, h:h + 1], scale=1.0)
            k.act(ea[:], ea[:], AF.Ln, [ea], [ea], bias=1.0)
            k.ve(lambda: nc.vector.tensor_tensor(out=gcl[:], in0=ea[:], in1=nA[:, d, h:h + 1], op=ALU.mult), [ea, nA], [gcl])
            k.act(beta[:], ab[:, 2 * nD + d * nD + h:2 * nD + d * nD + h + 1], AF.Sigmoid, [ab], [beta])
            k.ve(lambda: nc.vector.tensor_scalar_mul(out=nbeta[:], in0=beta[:], scalar1=-1.0), [beta], [nbeta])
            k.ve(lambda: nc.vector.tensor_scalar_mul(out=Dg[:], in0=cst[:, INC, :], scalar1=gcl[:, 0:1]), [cst, gcl], [Dg])
            k.ve(lambda: nc.vector.tensor_scalar_mul(out=Db[:], in0=cst[:, 0, :], scalar1=beta[:, 0:1]), [cst, beta], [Db])
            pgc = k.psum()
            k.mm(pgc[:, 0:128], cst[:, 1, :], Dg[:], True, True, [cst, Dg], [pgc], inc=False)
            k.mm(pgc[:, 128:129], cst[:, INC, :], gcl[:], True, True, [cst, gcl], [pgc], inc=False)
            k.mm(pgc[:, 129:130], cst[:, 1, :], gcl[:], True, True, [cst, gcl], [pgc], inc=False)
            k.mm(pgc[:, 256:384], cst[:, 1, :], Db[:], True, True, [cst, Db], [pgc])
            k.act(gcs[:], pgc[:, 0:130], AF.Copy, [pgc], [gcs])
            k.act(brow[:], pgc[:, 256:384], AF.Copy, [pgc], [brow])
            k.ve(lambda: nc.vector.tensor_scalar_mul(out=ngc[:], in0=gcs[:, 128:129], scalar1=-1.0), [gcs], [ngc])
            k.ve(lambda: nc.vector.tensor_tensor(out=tm1[:], in0=gcs[:, 0:128], in1=cst[:, NEGI, :], op=ALU.add), [gcs, cst], [tm1])
            k.act(DTi[:], tm1[:], AF.Exp, [tm1, ngc], [DTi], bias=ngc[:, 0:1], scale=1.0)
            k.ve(lambda: nc.vector.scalar_tensor_tensor(out=tm2[:], in0=gcs[:, 0:128], scalar=-1.0, in1=cst[:, NEGIT, :], op0=ALU.mult,
                                                        op1=ALU.add), [gcs, cst], [tm2])
            k.act(Di[:], tm2[:], AF.Exp, [tm2, gcs], [Di], bias=gcs[:, 128:129], scale=1.0)
            pg = k.psum()
            k.mm(pg[:, 0:128], knb[:], knb[:], True, True, [knb], [pg], inc=False)
            k.mm(pg[:, 128:256], knb[:], qnb[:], True, True, [knb, qnb], [pg])
            k.ve(lambda: nc.vector.tensor_tensor(out=tm1[:], in0=DTi[:], in1=cst[:, STR, :], op=ALU.mult), [DTi, cst], [tm1])
            k.ve(lambda: nc.vector.tensor_tensor(out=tm1[:], in0=tm1[:], in1=brow[:], op=ALU.mult), [tm1, brow], [tm1])
            k.ve(lambda: nc.vector.scalar_tensor_tensor(out=YX[:, 0:128], in0=pg[:, 0:128], scalar=-1.0, in1=tm1[:], op0=ALU.mult,
                                                        op1=ALU.mult), [pg, tm1], [YX])
            k.ve(lambda: nc.vector.tensor_tensor(out=tm2[:], in0=Di[:], in1=cst[:, STRT, :], op=ALU.mult), [Di, cst], [tm2])
            k.ve(lambda: nc.vector.scalar_tensor_tensor(out=YX[:, 128:256], in0=pg[:, 0:128], scalar=nbeta[:, 0:1], in1=tm2[:],
                                                        op0=ALU.mult, op1=ALU.mult), [pg, nbeta, tm2], [YX])
            k.ve(lambda: nc.vector.tensor_tensor(out=attn[:], in0=pg[:, 128:256], in1=DTi[:], op=ALU.mult), [pg, DTi], [attn])
            bm = self.bmask
            for hf in range(2):
                k.ve(lambda: nc.vector.tensor_tensor(out=EXF[:, hf, :, :], in0=bm[:],
                                                     in1=YX[:, hf * 128:(hf + 1) * 128].unsqueeze(1).broadcast_to([128, 7, 128]),
                                                     op=ALU.mult), [bm, YX], [EXF])
            rc = 0
            k.act(DR[0][:, 0:128], cst[:, 0, :], AF.Copy, [cst], [DR[0]])
            k.act(DR[0][:, 128:256], cst[:, 0, :], AF.Copy, [cst], [DR[0]])
            for lv in range(7):
                Dc = DR[rc]
                pp = k.psum()
                k.mm(pp[:, 0:128], EXF[:, 0, lv, :], Dc[:, 0:128], True, True, [EXF, Dc], [pp], inc=False)
                k.mm(pp[:, 128:256], EXF[:, 1, lv, :], Dc[:, 128:256], True, True, [EXF, Dc], [pp])
                k.act(WZ[:], pp[:, 0:256], AF.Copy, [pp], [WZ])
                pz = k.psum()
                k.mm(pz[:, 0:128], Dc[:, 128:256], WZ[:, 0:128], True, True, [Dc, WZ], [pz], inc=False)
                k.mm(pz[:, 128:256], Dc[:, 0:128], WZ[:, 128:256], True, True, [Dc, WZ], [pz])
                k.ve(lambda: nc.vector.tensor_tensor(out=DR[1 - rc][:], in0=pz[:, 0:256], in1=Dc[:], op=ALU.add), [pz, Dc], [DR[1 - rc]])
                rc = 1 - rc
            k.act(Rb[:], DR[rc][:, 128:256], AF.Copy, [DR[rc]], [Rb])
            R = Rb
            k.act(eg[:], gcs[:, 128:129], AF.Exp, [gcs], [eg])
            k.ve(lambda: nc.vector.tensor_tensor(out=bg[:], in0=beta[:], in1=eg[:], op=ALU.mult), [beta, eg], [bg])
            k.ve(lambda: nc.vector.tensor_scalar_mul(out=vbt[:], in0=ktv[:, 128:256], scalar1=beta[:, 0:1]), [ktv, beta], [vbt])
            k.ve(lambda: nc.vector.tensor_scalar_mul(out=kbe[:], in0=ktv[:, 0:128], scalar1=bg[:, 0:1]), [ktv, bg], [kbe])
            k.act(kstc[:], gcs[:, 128:129], AF.Exp, [gcs], [kstc], bias=gcs[:, 129:130], scale=-1.0)
            k.ve(lambda: nc.vector.tensor_scalar_mul(out=kst[:], in0=ktv[:, 0:128], scalar1=kstc[:, 0:1]), [ktv, kstc], [kst])
            k.act(egl[:], gcs[:, 129:130], AF.Exp, [gcs], [egl])
            k.act(erow[:], gcs[:, 0:128], AF.Exp, [gcs], [erow])
            k.ve(lambda: nc.vector.tensor_tensor(out=qdec[:], in0=qn[:], in1=erow[:], op=ALU.mult), [qn, erow], [qdec])
            pw = k.psum()
            k.mm(pw[:, 0:128], kbe[:], R[:], True, True, [kbe, R], [pw])
            k.act(nwT[:], pw[:, 0:128], AF.Identity, [pw], [nwT], scale=-1.0)
            pv = k.psum()
            k.mm(pv[:, 0:128], R[:], vbt[:], True, False, [R, vbt], [pv], inc=False)
            k.mm(pv[:, 0:128], nwT[:], Sb[:], False, True, [nwT, Sb], [pv])
            k.act(vnew[:], pv[:, 0:128], AF.Copy, [pv], [vnew])
            po = k.psum()
            k.mm(po[:, 0:128], Sb[:], qdec[:], True, False, [Sb, qdec], [po], inc=False)
            k.mm(po[:, 0:128], vnew[:], attn[:], False, True, [vnew, attn], [po])
            oi = 2 * J.nG + h
            if d == 0:
                k.act(ost[:], po[:, 0:128], AF.Copy, [po], [ost])
                k.dma(k.SP, J.OA[oi, :, a:a + 128], ost[:], J.OA, [ost], part=True)
            else:
                k.dma(k.SP, opv[:], J.OA[oi, :, a:a + 128], opv, [J.OA])
                k.ve(lambda: nc.vector.tensor_tensor(out=ost[:], in0=po[:, 0:128], in1=opv[:], op=ALU.add), [po, opv], [ost])
                k.act(osq[:], ost[:], AF.Square, [ost], [osq])
                pr = k.psum()
                k.mm(pr[:, 0:128], cst[:, 1, :], osq[:], True, True, [cst, osq], [pr])
                k.act(rstd[:], pr[:, 0:128], AF.Sqrt, [pr], [rstd], bias=EPS, scale=1.0 / 128)
                k.ve(lambda: nc.vector.reciprocal(out=rstd[:], in_=rstd[:]), [rstd], [rstd])
                k.dma(k.SP, gT[:], PF[pfi["d_g"] + h, :, a:a + 128], gT, [PF])
                k.act(gT[:], gT[:], AF.Silu, [gT], [gT])
                k.ve(lambda: nc.vector.scalar_tensor_tensor(out=t1[:], in0=ost[:], scalar=dng[:, l:l + 1], in1=rstd[:], op0=ALU.mult,
                                                            op1=ALU.mult), [ost, dng, rstd], [t1])
                k.ve(lambda: nc.vector.tensor_tensor(out=brs[:], in0=t1[:], in1=gT[:], op=ALU.mult), [t1, gT], [brs])
                k.dma(k.SP, J.BR[J.nbr + h, :, a:a + 128], brs[:], J.BR, [brs], part=True)
            pS = k.psum()
            k.mm(pS[:, 0:128], kst[:], vnew[:], True, True, [kst, vnew], [pS])
            k.ve(lambda: nc.vector.scalar_tensor_tensor(out=S[:], in0=S[:], scalar=egl[:, 0:1], in1=pS[:, 0:128], op0=ALU.mult,
                                                        op1=ALU.add), [S, egl, pS], [S])
            k.act(Sb[:], S[:], AF.Copy, [S], [Sb])

        for (s0, nts) in J.seqs:
            for d in range(2):
                for h in range(nD):
                    if J.kind == 'p':
                        k.ve(lambda: nc.vector.memset(S_l[h][:], 0.0), [], [S_l[h]])
                    else:
                        k.dma(k.SP, S_l[h][:], self.dr["sd_in"].t[l, d, h], S_l[h])
                    k.act(Sb_l[h][:], S_l[h][:], AF.Copy, [S_l[h]], [Sb_l[h]])
                order = list(range(nts)) if d == 0 else list(range(nts - 1, -1, -1))
                for t in order:
                    for h in range(nD):
                        step(sets[h % NSET], S_l[h], Sb_l[h], s0, nts, t, h, d)
                if J.kind == 'p':
                    so = self.dr["o_df" if d == 0 else "o_db"]
                    for h in range(nD):
                        k.dma(k.SP, so[l, s0 // 256, h], S_l[h][:], so, [S_l[h]], part=True)

    def swa(self, J, l, st_):
        k, nc = self.k, self.nc
        cst, cstb = self.cst, self.cstb
        sb = lambda n, sh, dt, dma=False: k.sb("sw_" + n, sh, dt, dma=dma, stack=st_)
        PF, PT, pfi, ptc = J.PF, J.PT, J.pfi, J.ptc
        L = J.seqs[0][1] * 128
        nt = L // 128
        ld = sb("ld", [128, 128], F32, True); cs = sb("cs", [128, 128], F32, True); sn = sb("sn", [128, 128], F32, True)
        rq = sb("rq", [128, 128], F32)
        kb = sb("kb", [128, L], BF16); vb = sb("vb", [128, nt, 128], BF16, True)
        qb = sb("qb", [128, 128], BF16); pT = [sb("pT%d" % j, [128, 128], BF16) for j in range(2)]
        den = sb("den", [128, 128], F32); ob = sb("ob", [128, 128], BF16)
        lat = (J.kind == 's')
        if lat:
            kc = sb("kc", [128, 512], BF16, True); vc = sb("vc", [128, 4, 128], BF16, True)
            k.dma(k.POOL, kc[:], self.dr["kc_in"].t[l], kc)
            k.dma(k.POOL, vc[:], self.dr["vc_in"].t[l], vc)

        def rope(dst, dst_t, a):
            k.dma(k.SP, cs[:], self.dr["cos"].t[:, a:a + 128], cs)
            k.dma(k.SP, sn[:], self.dr["sin"].t[:, a:a + 128], sn)
            pr = k.psum()
            k.mm(pr[:, 0:128], cst[:, 8, :], ld[:], True, True, [cst, ld], [pr])
            k.ve(lambda: nc.vector.tensor_tensor(out=rq[:], in0=pr[:, 0:128], in1=sn[:], op=ALU.mult), [pr, sn], [rq])
            k.ve(lambda: nc.vector.tensor_tensor(out=cs[:], in0=ld[:], in1=cs[:], op=ALU.mult), [ld, cs], [cs])
            k.ve(lambda: nc.vector.tensor_tensor(out=dst, in0=cs[:], in1=rq[:], op=ALU.add), [cs, rq], [dst_t])

        for (s0, nts) in J.seqs:
            for kv in range(J.nKV):
                for t in range(nts):
                    a = s0 + t * 128
                    k.dma(k.SP, ld[:], PF[pfi["s_k"] + kv, :, a:a + 128], ld, [PF])
                    if lat:
                        rope(kb[:, t * 128:(t + 1) * 128], kb, a)
                    else:
                        k.act(kb[:, t * 128:(t + 1) * 128], ld[:], AF.Copy, [ld], [kb])
                    k.dma(k.POOL, vb[:, t, :], PT[:, a // 128, ptc["s_v"] + kv * 128:ptc["s_v"] + (kv + 1) * 128], vb, [PT], part=True)
                for hq in range(kv * (J.nQ // J.nKV), (kv + 1) * (J.nQ // J.nKV)):
                    for t in range(nts):
                        a = s0 + t * 128
                        k.dma(k.SP, ld[:], PF[pfi["s_q"] + hq, :, a:a + 128], ld, [PF])
                        if lat:
                            rope(qb[:], qb, a)
                            k.ve(lambda: nc.vector.tensor_scalar_mul(out=qb[:], in0=qb[:], scalar1=128.0 ** -0.5), [qb], [qb])
                        else:
                            k.act(qb[:], ld[:], AF.Identity, [ld], [qb], scale=128.0 ** -0.5)
                        if lat:
                            keys = [(kb[:, j * 128:(j + 1) * 128], vb[:, j, :], (3 if j < t else (2 if j > t else None)), kb, vb)
                                    for j in (t - 1, t, t + 1) if 0 <= j < nts]
                            keys += [(kc[:, j * 128:(j + 1) * 128], vc[:, j, :], None, kc, vc) for j in range(4)]
                        else:
                            keys = [(kb[:, j * 128:(j + 1) * 128], vb[:, j, :], None, kb, vb) for j in range(nts)]
                        k.ps_limit = 6
                        pso = k.ps[6]
                        psd = k.ps[7]
                        for ki, (kap, vap, msk, kt_, vt_) in enumerate(keys):
                            pss = k.psum()
                            k.mm(pss[:, 0:128], kap, qb[:], True, True, [kt_, qb], [pss])
                            p_ = pT[ki % 2]
                            k.act(p_[:], pss[:, 0:128], AF.Exp, [pss], [p_])
                            if msk is not None:
                                k.ve(lambda: nc.vector.tensor_tensor(out=p_[:], in0=p_[:], in1=cstb[:, msk, :], op=ALU.mult), [p_, cstb], [p_])
                            last = (ki == len(keys) - 1)
                            k.mm(pso[:, 0:128], vap, p_[:], ki == 0, last, [vt_, p_], [pso], inc=True)
                            k.mm(psd[:, 0:128], cstb[:, 1, :], p_[:], ki == 0, last, [cstb, p_], [psd], inc=True)
                        k.ve(lambda: nc.vector.tensor_scalar(out=den[:], in0=psd[:, 0:128], scalar1=J.esink[:, hq:hq + 1], scalar2=None, op0=ALU.add),
                             [psd, J.esink], [den])
                        k.ve(lambda: nc.vector.reciprocal(out=den[:], in_=den[:]), [den], [den])
                        k.ve(lambda: nc.vector.tensor_tensor(out=ob[:], in0=pso[:, 0:128], in1=den[:], op=ALU.mult), [pso, den], [ob])
                        k.dma(k.SP, J.BR[2 * J.nbr + hq, :, a:a + 128], ob[:], J.BR, [ob], part=True)
        k.ps_limit = 8

    def phase_c(self, l):
        k, nc = self.k, self.nc
        win = self.dr["w_in"].t[l]
        wbr = self.dr["w_branch"].t[l]
        wo = self.dr["w_o"].t[l]
        mc = self.modc[l]
        with ExitStack() as st_:
            hb = k.sb("pc_h", [128, 16, 512], BF16, dma=True, stack=st_)
            brb = k.sb("pc_b", [128, 24, 512], BF16, dma=True, stack=st_)
            brq = k.sb("pc_q", [128, 24, 512], BF16, dma=True, stack=st_)
            mixT = k.sb("pc_m", [128, 16, 512], BF16, stack=st_)
            x_sb = k.sb("pc_x", [128, 16, 512], F32, dma=True, stack=st_)
            sg = k.sb("pc_sg", [128, 512], F32, stack=st_)
            acc = k.sb("pc_acc", [128, 512], F32, stack=st_)
            tmp = k.sb("pc_tmp", [128, 512], F32, stack=st_)
            for blk in range(TT // 512):
                t0 = blk * 512
                which = 0 if t0 < TP else 1
                if which == 1 and self.cfg.get("prompt_only", False):
                    continue
                k.dma(k.SP, hb[:, :, :], self.HT[:, :, t0:t0 + 512].rearrange("c p t -> p c t"), hb, [self.HT])
                if which == 0:
                    k.dma(k.SP, brb[:, :, :], self.dr["BR_p"][:, :, t0:t0 + 512].rearrange("c p t -> p c t"), brb, [self.dr["BR_p"]])
                else:
                    ts0 = t0 - TP
                    BG = self.BRG
                    for q in range(4):
                        for n in range(3):
                            for r in range(4):
                                src = BG[2 * n:2 * n + 2, r, :, q * 1024 + ts0:q * 1024 + ts0 + 512].rearrange("c p t -> p c t")
                                dst = brq[:, n * 8 + 2 * r:n * 8 + 2 * r + 2, :]
                                k.dma(k.SP, dst, src, brq, [BG], part=(n + r > 0))
                        if q == 0:
                            k.ve(lambda: nc.vector.tensor_scalar_mul(out=brb[:], in0=brq[:], scalar1=self.oh[:, 0:1]), [brq, self.oh], [brb])
                        else:
                            k.ve(lambda: nc.vector.scalar_tensor_tensor(out=brb[:], in0=brq[:], scalar=self.oh[:, q:q + 1], in1=brb[:], op0=ALU.mult,
                                                                        op1=ALU.add), [brq, self.oh, brb], [brb])
                for m in range(16):
                    s = self.next_wslot()
                    v = s[:, 0:16 * 384].rearrange("p (c n) -> p c n", c=16)
                    for n in range(3):
                        c0 = MG0 + n * D + m * 128
                        k.dma(k.POOL, v[:, :, n * 128:(n + 1) * 128], win[:, c0:c0 + 128].rearrange("(c p) n -> p c n", p=128), s, part=(n > 0))
                    s2 = self.next_oslot()
                    v2 = s2[:, 0:24 * 128].rearrange("p (c n) -> p c n", c=24)
                    for n in range(3):
                        k.dma(k.POOL, v2[:, n * 8:(n + 1) * 8, :], wbr[n, :, m * 128:(m + 1) * 128].rearrange("(c p) n -> p c n", p=128), s2, part=(n > 0))
                    for n in range(3):
                        pg = k.psum()
                        pu = k.psum()
                        for c in range(16):
                            k.mm(pg[:], v[:, c, n * 128:(n + 1) * 128], hb[:, c, :], c == 0, c == 15, [s, hb], [pg])
                        for c in range(8):
                            k.mm(pu[:], v2[:, n * 8 + c, :], brb[:, n * 8 + c, :], c == 0, c == 7, [s2, brb], [pu])
                        k.act(sg[:], pg[:], AF.Sigmoid, [pg], [sg])
                        if n == 0:
                            k.ve(lambda: nc.vector.tensor_tensor(out=acc[:], in0=sg[:], in1=pu[:], op=ALU.mult), [sg, pu], [acc])
                        else:
                            k.ve(lambda: nc.vector.tensor_tensor(out=tmp[:], in0=sg[:], in1=pu[:], op=ALU.mult), [sg, pu], [tmp])
                            k.ve(lambda: nc.vector.tensor_tensor(out=acc[:], in0=acc[:], in1=tmp[:], op=ALU.add), [acc, tmp], [acc])
                    k.act(mixT[:, m, :], acc[:], AF.Copy, [acc], [mixT])
                k.dma(k.SP, x_sb[:, :, :], self.xs[:, :, t0:t0 + 512].rearrange("c p t -> p c t"), x_sb, [self.xs])
                for m in range(16):
                    s, v = self.load_w(wo, m * 128, 128)
                    py = k.psum()
                    for c in range(16):
                        k.mm(py[:], v[:, c, :], mixT[:, c, :], c == 0, c == 15, [s, mixT], [py])
                    gt = mc[:, 5 * 16 + m, which:which + 1]
                    k.ve(lambda: nc.vector.scalar_tensor_tensor(out=x_sb[:, m, :], in0=py[:], scalar=gt, in1=x_sb[:, m, :], op0=ALU.mult, op1=ALU.add),
                         [py, mc, x_sb], [x_sb])
                k.dma(k.SP, self.xs[:, :, t0:t0 + 512].rearrange("c p t -> p c t"), x_sb[:, :, :], self.xs, [x_sb], part=True)
            k.barrier()

    def setup_mixer(self):
        k, nc = self.k, self.nc
        self.HT = self.dint("HT", [16, 128, TT], BF16)
        self.HSRC = self.dint("HSRC", [16 * 128, TS], BF16)
        self.HGd = self.dint("HGd", [4 * 16 * 128, TS], BF16)
        self.HG = T(self.HGd.t.rearrange("(q r c p) t -> q r c p t", q=4, r=4, c=4), self.HGd.buf)
        self.BRSd = self.dint("BRSd", [6 * 128, LS], BF16)
        self.dr["BRS"] = T(self.BRSd.t.rearrange("(c p) t -> c p t", c=6), self.BRSd.buf)
        self.BRGd = self.dint("BRGd", [4 * 6 * 128, LS], BF16)
        self.BRG = T(self.BRGd.t.rearrange("(c r p) t -> c r p t", c=6, r=4), self.BRGd.buf)
        self.gng = k.sb("gng_sb", [128, NL, 2], F32, dma=True)
        k.dma(k.SP, self.gng[:], self.din("gngT", [128, NL, 2])[:, :, :], self.gng)
        self.dng = k.sb("dng_sb", [128, NL], F32, dma=True)
        k.dma(k.SP, self.dng[:], self.din("dngT", [128, NL])[:, :], self.dng)
        self.oh = k.sb("oh_sb", [128, 4], F32, dma=True)
        k.dma(k.SP, self.oh[:], self.din("oh", [128, 4])[:, :], self.oh)
        for kind, nG, nD, nQ in (("p", 4, 8, 8), ("s", 1, 2, 2)):
            self.din("gw2_" + kind, [NL, 16, 2, nG * 128])
            self.din("gb_" + kind, [NL, 1, 2, nG * 128])
            self.din("cw_" + kind, [NL, 128, 3 * nD, 5])
            self.din("dtb_" + kind, [NL, 1, 2 * nD])
            self.din("dal_" + kind, [NL, 1, 2 * nD])
            self.din("snk_" + kind, [NL, 1, nQ])
        self.din("w_in_s", [NL, D, mixer_cols(1, 2, 2, 1)["_tot"]])
        self.din("sg_in", [NL, 2, 128, 256])
        self.din("sd_in", [NL, 2, 2, 128, 128])
        self.din("kc_in", [NL, 128, 512])
        self.din("vc_in", [NL, 128, 4, 128])
        self.din("cos", [128, LS])
        self.din("sin", [128, LS])
        self.dout("o_k", [NL, TP, 256])
        self.dout("o_v", [NL, TP, 256])
        self.dout("o_gf", [NL, 4, 4, 128, 256])
        self.dout("o_gb", [NL, 4, 4, 128, 256])
        self.dout("o_df", [NL, 4, 8, 128, 128])
        self.dout("o_db", [NL, 4, 8, 128, 128])

    def collective(self, src, dst, rows_per):
        k, nc = self.k, self.nc
        E = k.POOL
        if src.buf.w is not None:
            k.wait(E, src.buf.w)
        for ev in list(dst.buf.r.values()):
            k.wait(E, ev)
        g = dst.buf.semg
        rows = src.th.ap().shape[0]
        for q in range(rows // rows_per):
            nc.gpsimd.collective_compute("AllGather", ALU.bypass, replica_groups=[[0, 1, 2, 3], [4, 5, 6, 7]],
                                         ins=[src.th.ap()[q * rows_per:(q + 1) * rows_per, :]],
                                         outs=[dst.th.ap()[q * 4 * rows_per:(q + 1) * 4 * rows_per, :]]).then_inc(g.sem)
            g.cnt += 1
        ev = ('d', g)
        src.buf.r[g] = ev
        dst.buf.w = ev
        dst.buf.r = {}

    def job_consts(self, J, l, st_):
        k, nc = self.k, self.nc
        kd = J.kind
        nG, nD, nQ = J.nG, J.nD, J.nQ
        sb = lambda n, sh, dma=True: k.sb("jc_" + kd + n, sh, F32, dma=dma, stack=st_)
        J.gw2 = sb("w2", [16, 2, nG * 128]); k.dma(k.SP, J.gw2[:], self.dr["gw2_" + kd].t[l], J.gw2)
        J.gb = sb("gb", [1, 2, nG * 128]); k.dma(k.SP, J.gb[:], self.dr["gb_" + kd].t[l], J.gb)
        J.cw = sb("cw", [128, 3 * nD, 5]); k.dma(k.SP, J.cw[:], self.dr["cw_" + kd].t[l], J.cw)
        J.dtb = sb("dtb", [128, 2, nD])
        k.dma(k.SP, J.dtb[:].rearrange("p d h -> p (d h)"), self.dr["dtb_" + kd].t[l].broadcast_to([128, 2 * nD]), J.dtb)
        J.nA = sb("nA", [128, 2, nD])
        k.dma(k.SP, J.nA[:].rearrange("p d h -> p (d h)"), self.dr["dal_" + kd].t[l].broadcast_to([128, 2 * nD]), J.nA)
        k.act(J.nA[:], J.nA[:], AF.Exp, [J.nA], [J.nA])
        k.ve(lambda: nc.vector.tensor_scalar_mul(out=J.nA[:], in0=J.nA[:], scalar1=-1.0), [J.nA], [J.nA])
        J.esink = sb("es", [128, nQ])
        k.dma(k.SP, J.esink[:], self.dr["snk_" + kd].t[l].broadcast_to([128, nQ]), J.esink)
        k.act(J.esink[:], J.esink[:], AF.Exp, [J.esink], [J.esink])

    def mixer(self, l):
        k, nc = self.k, self.nc
        with ExitStack() as st_:
            x_sb = k.sb("mx", [128, 16, 512], F32, dma=True, stack=st_)
            hT = k.sb("mh", [128, 16, 512], BF16, stack=st_)
            st = {"sq": [k.sb("msq%d" % j, [128, 512], F32, stack=st_) for j in range(2)],
                  "rstd": k.sb("mrs", [128, 512], F32, stack=st_)}
            for blk in range(TT // 512):
                t0 = blk * 512
                which = 0 if t0 < TP else 1
                self.norm_block(l, 1, t0, 512, which, x_sb, hT, st)
                k.dma(k.SP, self.HT[:, :, t0:t0 + 512].rearrange("c p t -> p c t"), hT[:, :, :], self.HT, [hT], part=True)
                if which == 1:
                    k.dma(k.SP, self.HSRC[:, t0 - TP:t0 - TP + 512].rearrange("(c p) t -> p c t", p=128), hT[:, :, :], self.HSRC, [hT], part=True)
            k.barrier()
        po = self.cfg.get("prompt_only", False)
        if not po:
            self.collective(self.HSRC, self.HGd, 512)
        for kind in (("p",) if po else ("p", "s")):
            J = self.mk_job(kind, l)
            if kind == "p":
                self.proj(J, lambda tb: [(0, 16, self.HT[:, :, 0:TP].rearrange("c p t -> p c t"))])
                for nm in ("k", "v"):
                    o = self.dr["o_" + nm]
                    c0 = J.ptc["s_" + nm]
                    k.dma(k.SP, o[l].rearrange("(n p) c -> p n c", p=128), J.PT[:, :, c0:c0 + 256], o, [J.PT], part=True)
            else:
                self.proj(J, lambda tb: [(4 * q, 4 * q + 4, self.HG[q, tb].rearrange("c p t -> p c t")) for q in range(4)])
            with ExitStack() as st_:
                self.job_consts(J, l, st_)
                with ExitStack() as s2:
                    self.gla(J, l, s2)
                    k.barrier()
                with ExitStack() as s2:
                    self.dn(J, l, s2)
                    k.barrier()
                with ExitStack() as s2:
                    self.swa(J, l, s2)
                    k.barrier()
        if not po:
            self.collective(self.BRSd, self.BRGd, 128)
        self.phase_c(l)


def make_consts():
    p = np.arange(128)[:, None]
    f = np.arange(128)[None, :]
    c = np.zeros((128, 9, 128), np.float32)
    c[:, 0] = (p == f)
    c[:, 1] = 1.0
    c[:, 2] = (p <= f)
    c[:, 3] = (p >= f)
    c[:, 4] = (p < f)
    c[:, 5] = (p > f)
    c[:, 6] = ((p <= f) - 1.0) * 30000.0
    c[:, 7] = ((p >= f) - 1.0) * 30000.0
    R = np.zeros((128, 128), np.float32)
    for base in (0, 64):
        for i in range(32):
            R[base + i, base + 32 + i] = -1.0
            R[base + 32 + i, base + i] = 1.0
    c[:, 8] = R.T
    return c


def make_bmask():
    i = np.arange(128)[:, None]
    j = np.arange(128)[None, :]
    m = np.zeros((128, 7, 128), np.float32)
    for lv in range(7):
        b = 1 << lv
        m[:, lv, :] = ((i // (2 * b)) == (j // (2 * b))) & (((i % (2 * b)) >= b) != ((j % (2 * b)) >= b))
    return m


def build_nc(cfg):
    nc = bass.Bass("TRN2", target_bir_lowering=False)
    es = ExitStack()
    with es:
        P = Prog(nc, es, cfg)
        P.din("w_mod", [NL, D, 9 * D])
        P.din("ffn_in", [NL, 2, D, 2 * DFF])
        P.din("ffn_out", [NL, 2, DFF, D])
        xTd = P.din("xT", [16, 128, TT])
        P.dout("yT", [16, 128, TT])
        blk = es.enter_context(nc.Block())
        P.setup()
        if cfg.get("mixer", False):
            P.din("w_in", [NL, D, DIN])
            P.din("w_branch", [NL, 3, 1024, D])
            P.din("w_o", [NL, D, D])
            P.setup_mixer()
        k = P.k
        with ExitStack() as st_:
            tmp = k.sb("ldx", [128, 16, 512], F32, dma=True, stack=st_)
            for b in range(TT // 512):
                k.dma(k.SP, tmp[:, :, :], xTd[:, :, b * 512:(b + 1) * 512].rearrange("c p t -> p c t"), tmp)
                k.dma(k.SP, P.xs[:, :, b * 512:(b + 1) * 512].rearrange("c p t -> p c t"), tmp[:, :, :], P.xs, [tmp], part=True)
            k.barrier()
        for l in range(cfg["layers"]):
            P.mods(l)
            if cfg.get("ffn1", True):
                P.ffn(l, 0)
            if cfg.get("mixer", False):
                P.mixer(l)
            if cfg.get("ffn2", False):
                P.ffn(l, 1)
        P.final()
        k.barrier()
    return nc


FULL_CFG = {"layers": NL, "ffn1": True, "mixer": True, "ffn2": True}
_NC_CACHE = {}


def _wins_cols(p):
    r = lambda a, b: list(range(a, b))
    c = []
    c += r(128 * p, 128 * p + 128)
    c += r(512 + 128 * p, 512 + 128 * p + 128)
    c += r(1024 + 256 * p, 1024 + 256 * p + 256)
    c += r(2048 + 256 * p, 2048 + 256 * p + 256)
    c += r(3072, 3104)
    for base in (3104, 4128, 5152):
        c += r(base + 256 * p, base + 256 * p + 256)
    for base in (6176, 6184, 6192, 6200):
        c += r(base + 2 * p, base + 2 * p + 2)
    c += r(6208 + 256 * p, 6208 + 256 * p + 256)
    c += r(7232 + 256 * p, 7232 + 256 * p + 256)
    kv = p // 2
    c += r(8256 + 128 * kv, 8256 + 128 * kv + 128)
    c += r(8512 + 128 * kv, 8512 + 128 * kv + 128)
    return np.array(c)


def kernel(x_prompt, x_sample, cache_attn_k, cache_attn_v, state_gla_fwd, state_gla_bwd, state_dn_fwd, state_dn_bwd, c, c_ctx,
           w_mod, b_mod, norm_g, ffn_in, ffn_out, w_in, gla_w2, gla_b, gla_norm_g, dn_conv, dn_a_log, dn_dt_bias, dn_norm_g,
           swa_sink, w_branch, w_o, final_norm_g):
    f = lambda a: np.ascontiguousarray(np.asarray(a, dtype=np.float32))
    (x_prompt, x_sample, cache_attn_k, cache_attn_v, state_gla_fwd, state_gla_bwd, state_dn_fwd, state_dn_bwd, c, c_ctx, w_mod, b_mod,
     norm_g, ffn_in, ffn_out, w_in, gla_w2, gla_b, gla_norm_g, dn_conv, dn_a_log, dn_dt_bias, dn_norm_g, swa_sink, w_branch, w_o,
     final_norm_g) = [f(a) for a in (x_prompt, x_sample, cache_attn_k, cache_attn_v, state_gla_fwd, state_gla_bwd, state_dn_fwd,
                                     state_dn_bwd, c, c_ctx, w_mod, b_mod, norm_g, ffn_in, ffn_out, w_in, gla_w2, gla_b, gla_norm_g,
                                     dn_conv, dn_a_log, dn_dt_bias, dn_norm_g, swa_sink, w_branch, w_o, final_norm_g)]
    in_maps = _prep(x_prompt, x_sample, cache_attn_k, cache_attn_v, state_gla_fwd, state_gla_bwd, state_dn_fwd, state_dn_bwd, c, c_ctx,
                    w_mod, b_mod, norm_g, ffn_in, ffn_out, w_in, gla_w2, gla_b, gla_norm_g, dn_conv, dn_a_log, dn_dt_bias, dn_norm_g,
                    swa_sink, w_branch, w_o, final_norm_g)
    if "nc" not in _NC_CACHE:
        _NC_CACHE["nc"] = build_nc(FULL_CFG)
    nc = _NC_CACHE["nc"]
    res = run_bass_kernel_spmd(nc, in_maps, core_ids=list(range(8)))
    return _post(res.results)


def _prep(x_prompt, x_sample, cache_attn_k, cache_attn_v, state_gla_fwd, state_gla_bwd, state_dn_fwd, state_dn_bwd, c, c_ctx,
          w_mod, b_mod, norm_g, ffn_in, ffn_out, w_in, gla_w2, gla_b, gla_norm_g, dn_conv, dn_a_log, dn_dt_bias, dn_norm_g,
          swa_sink, w_branch, w_o, final_norm_g):
    f = lambda a: np.ascontiguousarray(np.asarray(a, dtype=np.float32))
    tpos = np.arange(LS)
    inv = (10000.0 ** (-np.arange(32, dtype=np.float32) / 32)).astype(np.float32)
    pos = np.where(np.arange(128)[:, None] < 64, (tpos // 64)[None, :], (tpos % 64)[None, :]).astype(np.float32)
    ang = (pos * inv[np.arange(128) % 32][:, None]).astype(np.float32)
    cos_t, sin_t = f(np.cos(ang)), f(np.sin(ang))
    shared = {
        "cst": make_consts(), "bmask": make_bmask(), "w_mod": w_mod, "ffn_in": ffn_in, "ffn_out": ffn_out, "w_in": w_in, "w_branch": w_branch, "w_o": w_o,
        "bmT": f(b_mod.reshape(NL, 144, 128).transpose(2, 0, 1)),
        "ngT": f(np.concatenate([norm_g.reshape(6, D), final_norm_g[None]], 0).reshape(7, 16, 128).transpose(2, 0, 1)),
        "gngT": f(gla_norm_g.reshape(NL, 2, 128).transpose(2, 0, 1)), "dngT": f(dn_norm_g.T),
        "gw2_p": f(gla_w2.transpose(0, 2, 1, 3)), "gb_p": f(gla_b[:, None]),
        "cw_p": f(dn_conv.reshape(NL, 5, 24, 128).transpose(0, 3, 2, 1)),
        "dtb_p": f(dn_dt_bias.reshape(NL, 1, 16)), "dal_p": f(dn_a_log.reshape(NL, 1, 16)), "snk_p": f(swa_sink[:, None, :]),
        "cos": cos_t, "sin": sin_t,
    }
    in_maps = []
    for r in range(8):
        gi, p = r // 4, r % 4
        x = np.concatenate([x_prompt[4 * r:4 * r + 4].reshape(TP, D), x_sample[gi, TS * p:TS * (p + 1)]], 0)
        m = dict(shared)
        m["xT"] = f(x.T.reshape(16, 128, TT))
        m["cv"] = f(np.stack([c_ctx, c[gi]], -1).reshape(16, 128, 2).transpose(1, 0, 2))
        oh = np.zeros((128, 4), np.float32)
        oh[:, p] = 1.0
        m["oh"] = oh
        m["gw2_s"] = f(shared["gw2_p"][:, :, :, 128 * p:128 * p + 128])
        m["gb_s"] = f(shared["gb_p"][:, :, :, 128 * p:128 * p + 128])
        blocks = [2 * p, 2 * p + 1, 8 + 2 * p, 9 + 2 * p, 16 + 2 * p, 17 + 2 * p]
        m["cw_s"] = f(shared["cw_p"][:, :, blocks, :])
        m["dtb_s"] = f(dn_dt_bias[:, :, 2 * p:2 * p + 2].reshape(NL, 1, 4))
        m["dal_s"] = f(dn_a_log[:, :, 2 * p:2 * p + 2].reshape(NL, 1, 4))
        m["snk_s"] = f(swa_sink[:, None, 2 * p:2 * p + 2])
        m["w_in_s"] = f(w_in[:, :, _wins_cols(p)])
        m["sg_in"] = f(np.stack([state_gla_fwd[gi, :, p], state_gla_bwd[gi, :, p]], 1))
        m["sd_in"] = f(np.stack([state_dn_fwd[gi, :, 2 * p:2 * p + 2], state_dn_bwd[gi, :, 2 * p:2 * p + 2]], 1))
        m["kc_in"] = f(cache_attn_k[gi, :, :, p // 2, :].transpose(0, 2, 1))
        m["vc_in"] = f(cache_attn_v[gi, :, :, p // 2, :].reshape(NL, 4, 128, 128).transpose(0, 2, 1, 3))
        in_maps.append(m)
    return in_maps


def _post(results):
    B = 32
    y_prompt = np.zeros((B, 256, D), np.float32)
    y_sample = np.zeros((2, LS, D), np.float32)
    nk = np.zeros((B, NL, 256, 2, 128), np.float32)
    nv = np.zeros((B, NL, 256, 2, 128), np.float32)
    gf = np.zeros((B, NL, 4, 128, 256), np.float32)
    gb_ = np.zeros((B, NL, 4, 128, 256), np.float32)
    df = np.zeros((B, NL, 8, 128, 128), np.float32)
    db = np.zeros((B, NL, 8, 128, 128), np.float32)
    for r in range(8):
        gi, p = r // 4, r % 4
        o = results[r]
        y = np.asarray(o["yT"]).reshape(D, TT).T
        y_prompt[4 * r:4 * r + 4] = y[0:TP].reshape(4, 256, D)
        y_sample[gi, TS * p:TS * (p + 1)] = y[TP:]
        nk[4 * r:4 * r + 4] = np.asarray(o["o_k"]).reshape(NL, 4, 256, 2, 128).transpose(1, 0, 2, 3, 4)
        nv[4 * r:4 * r + 4] = np.asarray(o["o_v"]).reshape(NL, 4, 256, 2, 128).transpose(1, 0, 2, 3, 4)
        gf[4 * r:4 * r + 4] = np.asarray(o["o_gf"]).transpose(1, 0, 2, 3, 4)
        gb_[4 * r:4 * r + 4] = np.asarray(o["o_gb"]).transpose(1, 0, 2, 3, 4)
        df[4 * r:4 * r + 4] = np.asarray(o["o_df"]).transpose(1, 0, 2, 3, 4)
        db[4 * r:4 * r + 4] = np.asarray(o["o_db"]).transpose(1, 0, 2, 3, 4)
    return (y_prompt, y_sample, nk, nv, gf, gb_, df, db)
```

```python
import numpy as np
from contextlib import ExitStack
import concourse.bass as bass
import concourse.mybir as mybir
from concourse.bass_utils import run_bass_kernel_spmd

F32 = mybir.dt.float32
BF16 = mybir.dt.bfloat16
AF = mybir.ActivationFunctionType
ALU = mybir.AluOpType

D = 2048
KC = 16
DFF = 5504
HC = 43
NL = 2
TP = 1024
TS = 1024
TT = TP + TS
LS = 4096
EPS = 1e-6
DIN = 14912
MG0 = 8768


def mixer_cols(nG, nD, nQ, nKV):
    names = [("g_q", nG * 128), ("g_k", nG * 128), ("g_v", nG * 256), ("g_r", nG * 256), ("g_lf", 16), ("g_lb", 16),
             ("d_q", nD * 128), ("d_k", nD * 128), ("d_v", nD * 128), ("d_af", nD), ("d_ab", nD), ("d_bf", nD),
             ("d_bb", nD), ("d_g", nD * 128), ("s_q", nQ * 128), ("s_k", nKV * 128), ("s_v", nKV * 128)]
    off = {}
    o = 0
    for n, w in names:
        off[n] = o
        o += w
    off["_tot"] = o
    return off


class SemG:
    def __init__(self, sem):
        self.sem = sem
        self.cnt = 0


class Buf:
    def __init__(self, name, semg=None):
        self.name = name
        self.w = None
        self.r = {}
        self.semg = semg


class T:
    def __init__(self, t, buf, th=None):
        self.t = t
        self.buf = buf
        self.th = th

    def __getitem__(self, key):
        return self.t[key]


class Eng:
    def __init__(self, raw, sem, pe=False):
        self.raw = raw
        self.sem = sem
        self.cnt = 0
        self.seen = {}
        self.pe = pe


class KB:
    def __init__(self, nc, es):
        self.nc = nc
        self.es = es
        mk = lambda n: es.enter_context(nc.semaphore(n))
        self.PE = Eng(nc.tensor, mk("s_pe"), pe=True)
        self.ACT = Eng(nc.scalar, mk("s_act"))
        self.DVE = Eng(nc.vector, mk("s_dve"))
        self.POOL = Eng(nc.gpsimd, mk("s_pool"))
        self.SP = Eng(nc.sync, mk("s_sp"))
        self.engs = [self.PE, self.ACT, self.DVE, self.POOL, self.SP]
        self.semgs = []
        self.nsem = 0
        self.ps = []
        self.psi = 0
        self.semg_names = {}

    def semg(self, name=None):
        if name is not None and name in self.semg_names:
            return self.semg_names[name]
        g = SemG(self.es.enter_context(self.nc.semaphore("dq%d" % self.nsem)))
        self.nsem += 1
        self.semgs.append(g)
        if name is not None:
            self.semg_names[name] = g
        return g

    def buf(self, name, dma=False, semg=None):
        return Buf(name, semg if semg is not None else (self.semg(name) if dma else None))

    def sb(self, name, shape, dt, dma=False, semg=None, stack=None):
        self.uid = getattr(self, "uid", 0) + 1
        t = (stack or self.es).enter_context(self.nc.sbuf_tensor("%s_u%d" % (name, self.uid), shape, dt))
        return T(t, self.buf(name, dma, semg))

    def wait(self, E, ev):
        if ev[0] == 'c':
            _, e, v = ev
            if e is E and E.pe:
                return
            if E.seen.get(e, 0) >= v:
                return
            E.raw.wait_ge(e.sem, v)
            E.seen[e] = v
        else:
            g = ev[1]
            v = g.cnt
            if E.seen.get(g, 0) >= v:
                return
            E.raw.wait_ge(g.sem, v)
            E.seen[g] = v

    def op(self, E, fn, R=(), W=(), inc=True):
        for t in R:
            b = t.buf
            if b.w is not None:
                self.wait(E, b.w)
        for t in W:
            b = t.buf
            if b.w is not None:
                self.wait(E, b.w)
            for ev in list(b.r.values()):
                self.wait(E, ev)
        ins = fn()
        if inc:
            E.cnt += 1
            ins.then_inc(E.sem, 1)
            v = E.cnt
        else:
            assert E.pe
            v = E.cnt + 1
        ev = ('c', E, v)
        for t in R:
            t.buf.r[E] = ev
        for t in W:
            t.buf.w = ev
            t.buf.r = {}
        return ins

    def dma(self, Q, out, in_, W, R=(), part=False):
        wb = W.buf
        for t in R:
            if t.buf.w is not None:
                self.wait(Q, t.buf.w)
        if wb.w is not None and not (part and wb.w[0] == 'd' and wb.w[1] is wb.semg):
            self.wait(Q, wb.w)
        for ev in list(wb.r.values()):
            self.wait(Q, ev)
        g = wb.semg
        Q.raw.dma_start(out=out, in_=in_).then_inc(g.sem, 16)
        g.cnt += 16
        ev = ('d', g)
        for t in R:
            t.buf.r[g] = ev
        wb.w = ev
        wb.r = {}

    def barrier(self):
        for E in self.engs:
            for e2 in self.engs:
                if e2 is not E and e2.cnt > 0:
                    self.wait(E, ('c', e2, e2.cnt))
            for g in self.semgs:
                if g.cnt > 0:
                    self.wait(E, ('d', g))

    def mm(self, out, lhsT, rhs, start, stop, R, W, inc=None):
        if inc is None:
            inc = stop
        return self.op(self.PE, lambda: self.nc.tensor.matmul(out, lhsT, rhs, start=start, stop=stop), R, W, inc=inc)

    def act(self, out, in_, func, R, W, bias=None, scale=None):
        kw = {}
        if bias is not None:
            kw["bias"] = bias
        if scale is not None:
            kw["scale"] = scale
        return self.op(self.ACT, lambda: self.nc.scalar.activation(out=out, in_=in_, func=func, **kw), R, W)

    def ve(self, fn, R, W):
        return self.op(self.DVE, fn, R, W)

    def psum(self):
        p = self.ps[self.psi % getattr(self, "ps_limit", len(self.ps))]
        self.psi += 1
        return p


class Prog:
    def __init__(self, nc, es, cfg):
        self.nc = nc
        self.es = es
        self.cfg = cfg
        self.k = KB(nc, es)
        self.dr = {}

    def din(self, name, shape, dt=F32):
        t = self.nc.dram_tensor(name, list(shape), dt, kind="ExternalInput")
        a = T(t.ap(), Buf(name), t)
        self.dr[name] = a
        return a

    def dout(self, name, shape, dt=F32):
        t = self.nc.dram_tensor(name, list(shape), dt, kind="ExternalOutput")
        a = T(t.ap(), self.k.buf(name, dma=True), t)
        self.dr[name] = a
        return a

    def dint(self, name, shape, dt=F32):
        t = self.nc.dram_tensor(name, list(shape), dt)
        a = T(t.ap(), self.k.buf(name, dma=True), t)
        self.dr[name] = a
        return a

    def setup(self):
        k, nc = self.k, self.nc
        for i in range(8):
            p = self.es.enter_context(nc.psum_tensor("ps%d" % i, [128, 512], F32))
            k.ps.append(T(p, Buf("ps%d" % i)))
        cst = self.din("cst", [128, 9, 128])
        self.cst = k.sb("cst_sb", [128, 9, 128], F32, dma=True)
        k.dma(k.SP, self.cst[:], cst[:, :, :], self.cst)
        self.cstb = k.sb("cstb_sb", [128, 9, 128], BF16)
        k.ve(lambda: nc.vector.tensor_copy(out=self.cstb[:], in_=self.cst[:]), [self.cst], [self.cstb])
        bmd = self.din("bmask", [128, 7, 128])
        self.bmask = k.sb("bmask_sb", [128, 7, 128], F32, dma=True)
        k.dma(k.SP, self.bmask[:], bmd[:, :, :], self.bmask)
        self.ones1 = k.sb("ones1", [1, 512], F32)
        k.ve(lambda: nc.vector.memset(self.ones1[:], 1.0), [], [self.ones1])
        ngd = self.din("ngT", [128, 7, 16])
        self.ng = k.sb("ng_sb", [128, 7, 16], F32, dma=True)
        k.dma(k.SP, self.ng[:], ngd[:, :, :], self.ng)
        cvd = self.din("cv", [128, 16, 2])
        cv = k.sb("cv_sb", [128, 16, 2], F32, dma=True)
        k.dma(k.SP, cv[:], cvd[:, :, :], cv)
        self.scv = k.sb("scv", [128, 16, 2], BF16)
        k.act(self.scv[:], cv[:], AF.Silu, [cv], [self.scv])
        bmd = self.din("bmT", [128, NL, 144])
        self.bm = k.sb("bm_sb", [128, NL, 144], F32, dma=True)
        k.dma(k.SP, self.bm[:], bmd[:, :, :], self.bm)
        self.modc = [k.sb("modc%d" % l, [128, 144, 2], F32) for l in range(NL)]
        self.wslot = [k.sb("wslot%d" % i, [128, 16 * 512], BF16, dma=True) for i in range(3)]
        self.wsi = 0
        self.oslot = [k.sb("oslot%d" % i, [128, 43 * 128], BF16, dma=True) for i in range(2)]
        self.osi = 0
        self.xs = self.dint("xs", [16, 128, TT])

    def next_wslot(self):
        s = self.wslot[self.wsi % 3]
        self.wsi += 1
        return s

    def next_oslot(self):
        s = self.oslot[self.osi % 2]
        self.osi += 1
        return s

    def load_w(self, wap, c0, width, slot=None):
        k = self.k
        s = slot or self.next_wslot()
        v = s[:, 0:16 * width].rearrange("p (c n) -> p c n", c=16)
        src = wap[:, c0:c0 + width].rearrange("(c p) n -> p c n", p=128)
        k.dma(k.POOL, v, src, s)
        return s, v

    def mods(self, l):
        k, nc = self.k, self.nc
        wm = self.dr["w_mod"].t
        for g in range(36):
            s, v = self.load_w(wm[l], g * 512, 512)
            ps = k.psum()
            for m in range(4):
                for c in range(16):
                    k.mm(ps[:, 2 * m:2 * m + 2], v[:, c, m * 128:(m + 1) * 128], self.scv[:, c, :], c == 0, c == 15,
                         [s, self.scv], [ps], inc=(c == 15))
            k.ve(lambda: nc.vector.tensor_tensor(
                out=self.modc[l][:, g * 4:(g + 1) * 4, :], in0=ps[:, 0:8].rearrange("p (m t) -> p m t", t=2),
                in1=self.bm[:, l, g * 4:(g + 1) * 4].unsqueeze(2).broadcast_to([128, 4, 2]), op=ALU.add),
                [ps, self.bm], [self.modc[l]])
        mc = self.modc[l]
        for i in range(3):
            sc = mc[:, (3 * i + 1) * 16:(3 * i + 2) * 16, :]
            k.ve(lambda: nc.vector.scalar_tensor_tensor(
                out=sc, in0=sc, scalar=1.0, in1=self.ng[:, l * 3 + i, :].unsqueeze(2).broadcast_to([128, 16, 2]),
                op0=ALU.add, op1=ALU.mult), [mc, self.ng], [mc])
            if i != 1:
                gt = mc[:, (3 * i + 2) * 16:(3 * i + 3) * 16, :]
                k.ve(lambda: nc.vector.tensor_scalar_mul(out=gt, in0=gt, scalar1=0.5), [mc], [mc])

    def norm_block(self, l, i, t0, ntok, which, x_sb, hT, st):
        k, nc = self.k, self.nc
        k.dma(k.SP, x_sb[:, :, 0:ntok], self.xs[:, :, t0:t0 + ntok].rearrange("c p t -> p c t"), x_sb, [self.xs])
        sq = st["sq"]
        ps = k.psum()
        for c in range(16):
            q = sq[c % 2]
            k.act(q[:, 0:ntok], x_sb[:, c, 0:ntok], AF.Square, [x_sb], [q])
            k.mm(ps[:, 0:ntok], self.cst[:, 1, :], q[:, 0:ntok], c == 0, c == 15, [self.cst, q], [ps], inc=True)
        rstd = st["rstd"]
        k.act(rstd[:, 0:ntok], ps[:, 0:ntok], AF.Sqrt, [ps], [rstd], bias=EPS, scale=1.0 / D)
        k.ve(lambda: nc.vector.reciprocal(out=rstd[:, 0:ntok], in_=rstd[:, 0:ntok]), [rstd], [rstd])
        for c in range(16):
            tmp = sq[c % 2]
            if i < 3:
                mc = self.modc[l]
                gcol = mc[:, (3 * i + 1) * 16 + c, which:which + 1]
                shcol = mc[:, (3 * i) * 16 + c, which:which + 1]
                k.ve(lambda: nc.vector.scalar_tensor_tensor(out=tmp[:, 0:ntok], in0=x_sb[:, c, 0:ntok], scalar=gcol,
                                                            in1=rstd[:, 0:ntok], op0=ALU.mult, op1=ALU.mult),
                     [x_sb, mc, rstd], [tmp])
                k.act(hT[:, c, 0:ntok], tmp[:, 0:ntok], AF.Identity, [tmp, mc], [hT], bias=shcol, scale=1.0)
            else:
                gcol = self.ng[:, 6, c:c + 1]
                k.ve(lambda: nc.vector.scalar_tensor_tensor(out=hT[:, c, 0:ntok], in0=x_sb[:, c, 0:ntok], scalar=gcol,
                                                            in1=rstd[:, 0:ntok], op0=ALU.mult, op1=ALU.mult),
                     [x_sb, self.ng, rstd], [hT])

    def ffn(self, l, f):
        k, nc = self.k, self.nc
        i = 0 if f == 0 else 2
        win = self.dr["ffn_in"].t[l, f]
        wout = self.dr["ffn_out"].t[l, f]
        with ExitStack() as st_:
            x_sb = k.sb("fx", [128, 16, 512], F32, dma=True, stack=st_)
            hT = k.sb("fh", [128, 16, 512], BF16, stack=st_)
            actT = k.sb("fa", [128, HC, 512], BF16, stack=st_)
            sg = k.sb("fsg", [128, 512], F32, stack=st_)
            st = {"sq": [k.sb("fsq%d" % j, [128, 512], F32, stack=st_) for j in range(2)],
                  "rstd": k.sb("frs", [128, 512], F32, stack=st_)}
            for blk in range(TT // 512):
                t0 = blk * 512
                which = 0 if t0 < TP else 1
                self.norm_block(l, i, t0, 512, which, x_sb, hT, st)
                ng_ = (DFF + 255) // 256
                for g in range(ng_):
                    h0 = g * 256
                    hw = min(256, DFF - h0)
                    s = self.next_wslot()
                    v = s[:, :].rearrange("p (c n) -> p c n", c=16)
                    k.dma(k.POOL, v[:, :, 0:hw], win[:, h0:h0 + hw].rearrange("(c p) n -> p c n", p=128), s)
                    k.dma(k.POOL, v[:, :, 256:256 + hw], win[:, DFF + h0:DFF + h0 + hw].rearrange("(c p) n -> p c n", p=128),
                          s, part=True)
                    for j in range(hw // 128):
                        pg = k.psum()
                        pu = k.psum()
                        for c in range(16):
                            k.mm(pg[:], v[:, c, j * 128:(j + 1) * 128], hT[:, c, :], c == 0, c == 15, [s, hT], [pg])
                        for c in range(16):
                            k.mm(pu[:], v[:, c, 256 + j * 128:256 + (j + 1) * 128], hT[:, c, :], c == 0, c == 15, [s, hT], [pu])
                        k.act(sg[:], pg[:], AF.Silu, [pg], [sg])
                        hc = (h0 // 128) + j
                        k.ve(lambda: nc.vector.tensor_tensor(out=actT[:, hc, :], in0=sg[:], in1=pu[:], op=ALU.mult),
                             [sg, pu], [actT])
                mc = self.modc[l]
                for m in range(16):
                    s = self.next_oslot()
                    v = s[:, :].rearrange("p (c n) -> p c n", c=HC)
                    k.dma(k.POOL, v, wout[:, m * 128:(m + 1) * 128].rearrange("(c p) n -> p c n", p=128), s)
                    py = k.psum()
                    for c in range(HC):
                        k.mm(py[:], v[:, c, :], actT[:, c, :], c == 0, c == HC - 1, [s, actT], [py])
                    gt = mc[:, (3 * i + 2) * 16 + m, which:which + 1]
                    k.ve(lambda: nc.vector.scalar_tensor_tensor(out=x_sb[:, m, :], in0=py[:], scalar=gt, in1=x_sb[:, m, :],
                                                                op0=ALU.mult, op1=ALU.add), [py, mc, x_sb], [x_sb])
                k.dma(k.SP, self.xs[:, :, t0:t0 + 512].rearrange("c p t -> p c t"), x_sb[:, :, :], self.xs, [x_sb], part=True)
            k.barrier()

    def final(self):
        k, nc = self.k, self.nc
        with ExitStack() as st_:
            x_sb = k.sb("nx", [128, 16, 512], F32, dma=True, stack=st_)
            y_sb = [k.sb("ny%d" % j, [128, 16, 512], F32, stack=st_) for j in range(2)]
            st = {"sq": [k.sb("nsq%d" % j, [128, 512], F32, stack=st_) for j in range(2)],
                  "rstd": k.sb("nrs", [128, 512], F32, stack=st_)}
            yT = self.dr["yT"]
            for blk in range(TT // 512):
                t0 = blk * 512
                y = y_sb[blk % 2]
                self.norm_block(0, 3, t0, 512, 0, x_sb, y, st)
                k.dma(k.SP, yT[:, :, t0:t0 + 512].rearrange("c p t -> p c t"), y[:, :, :], yT, [y], part=True)
            k.barrier()

    def mk_job(self, kind, l):
        J = type("J", (), {})()
        J.kind = kind
        if kind == 'p':
            J.nG, J.nD, J.nQ, J.nKV, J.L = 4, 8, 8, 2, TP
            J.seqs = [(s_ * 256, 2) for s_ in range(4)]
            J.W = self.dr["w_in"].t[l]
        else:
            J.nG, J.nD, J.nQ, J.nKV, J.L = 1, 2, 2, 1, LS
            J.seqs = [(0, 32)]
            J.W = self.dr["w_in_s"].t[l]
        J.cols = mixer_cols(J.nG, J.nD, J.nQ, J.nKV)
        nG, nD, nQ, nKV = J.nG, J.nD, J.nQ, J.nKV
        J.fq = [("g_q", nG * 128), ("g_k", nG * 128), ("g_r", nG * 256), ("g_lf", 16), ("g_lb", 16), ("d_q", nD * 128),
                ("d_k", nD * 128), ("d_v", nD * 128), ("d_g", nD * 128), ("s_q", nQ * 128), ("s_k", nKV * 128)]
        J.pfi = {}
        o = 0
        for n, w in J.fq:
            J.pfi[n] = o
            o += (w + 127) // 128
        J.NF = o
        J.tq = [("g_k", J.cols["g_k"], nG * 128), ("g_v", J.cols["g_v"], nG * 256), ("d_ab", J.cols["d_af"], 4 * nD),
                ("s_v", J.cols["s_v"], nKV * 128), ("s_k", J.cols["s_k"], nKV * 128)]
        J.ptc = {}
        o = 0
        for n, c0, w in J.tq:
            J.ptc[n] = o
            o += w
        J.NT = o
        sfx = "_" + kind
        if ("PF" + sfx) not in self.dr:
            self.dint("PF" + sfx, [J.NF, 128, J.L])
            self.dint("PT" + sfx, [128, J.L // 128, J.NT])
            self.dint("OA" + sfx, [nG * 2 + nD, 128, J.L])
            if kind == 'p':
                self.dint("BR" + sfx, [24, 128, J.L], BF16)
        J.PF, J.PT, J.OA = self.dr["PF" + sfx], self.dr["PT" + sfx], self.dr["OA" + sfx]
        J.BR = self.dr["BR_p"] if kind == 'p' else self.dr["BRS"]
        J.nbr = 8 if kind == 'p' else 2
        return J

    def proj(self, J, hsrc):
        k, nc = self.k, self.nc
        with ExitStack() as st_:
            hb = k.sb("pj_h", [128, 16, 1024], BF16, dma=True, stack=st_)
            stg = [k.sb("pj_s%d" % j, [128, 512], F32, stack=st_) for j in range(3)]
            si = 0
            for tb in range(J.L // 1024):
                for pi, (c0, c1, ap_) in enumerate(hsrc(tb)):
                    k.dma(k.SP, hb[:, c0:c1, :], ap_, hb, [self.HT, self.HG], part=(pi > 0))
                for name, width in J.fq:
                    col0 = J.cols[name]
                    done = 0
                    while done < width:
                        w = min(512, width - done)
                        s, v = self.load_w(J.W, col0 + done, w)
                        for j in range(0, w, 128):
                            rows = min(128, w - j)
                            idx = J.pfi[name] + (done + j) // 128
                            for half in range(2):
                                ps = k.psum()
                                for c in range(16):
                                    k.mm(ps[0:rows, :], v[:, c, j:j + rows], hb[:, c, half * 512:(half + 1) * 512], c == 0, c == 15,
                                         [s, hb], [ps])
                                sg = stg[si % 3]
                                si += 1
                                k.act(sg[0:rows, :], ps[0:rows, :], AF.Copy, [ps], [sg])
                                a = tb * 1024 + half * 512
                                k.dma(k.SP, J.PF[idx, 0:rows, a:a + 512], sg[0:rows, :], J.PF, [sg], part=True)
                        done += w
                for name, col0, width in J.tq:
                    done = 0
                    while done < width:
                        w = min(512, width - done)
                        s, v = self.load_w(J.W, col0 + done, w)
                        for tl in range(8):
                            ps = k.psum()
                            for c in range(16):
                                k.mm(ps[:, 0:w], hb[:, c, tl * 128:(tl + 1) * 128], v[:, c, 0:w], c == 0, c == 15, [s, hb], [ps])
                            sg = stg[si % 3]
                            si += 1
                            k.act(sg[:, 0:w], ps[:, 0:w], AF.Copy, [ps], [sg])
                            c0 = J.ptc[name] + done
                            k.dma(k.SP, J.PT[:, tb * 8 + tl, c0:c0 + w], sg[:, 0:w], J.PT, [sg], part=True)
                        done += w
            k.barrier()

    def gla(self, J, l, st_):
        k, nc = self.k, self.nc
        cst, cstb = self.cst, self.cstb
        NSET = min(3, J.nG)
        sets = []
        for si_ in range(NSET):
            sb = lambda n, sh, dt, dma=False, si_=si_: k.sb("gl%d_" % si_ + n, sh, dt, dma=dma, stack=st_,
                                                            semg=(k.semg("glset%d" % si_) if dma else None))
            lrT = sb("lr", [16, 128], F32, True); qT = sb("q", [128, 128], F32, True); kT = sb("k", [128, 128], F32, True)
            ktok = sb("kt", [128, 128], F32, True); vtok = sb("vt", [128, 256], F32, True)
            la = sb("la", [128, 128], F32); e1 = sb("e1", [128, 128], F32); e2 = sb("e2", [128, 128], F32)
            dcol = sb("dc", [128, 1], F32); tb_ = sb("tb", [128, 128], F32); est = sb("es", [128, 128], F32)
            qin = sb("qi", [128, 128], BF16); kin = sb("ki", [128, 128], BF16); kst = sb("ks", [128, 128], BF16)
            vb = sb("vb", [128, 256], BF16); at = sb("at", [128, 128], BF16)
            ost = sb("os", [128, 2, 128], F32); opv = sb("op", [128, 2, 128], F32, True); osq = sb("oq", [128, 2, 128], F32)
            rstd = sb("rs", [128, 128], F32); rT = sb("r", [128, 2, 128], F32, True); t1 = sb("t1", [128, 128], F32)
            brs = sb("br", [128, 2, 128], BF16)
            sets.append((lrT, qT, kT, ktok, vtok, la, e1, e2, dcol, tb_, est, qin, kin, kst, vb, at, ost, opv, osq, rstd, rT, t1, brs))
        sb = lambda n, sh, dt, dma=False: k.sb("gl_" + n, sh, dt, dma=dma, stack=st_, semg=(k.semg("glSS") if dma else None))
        S_l = [sb("S%d" % h_, [128, 256], F32, True) for h_ in range(J.nG)]
        Sb_l = [sb("Sb%d" % h_, [128, 256], BF16) for h_ in range(J.nG)]
        PF, PT, pfi, ptc = J.PF, J.PT, J.pfi, J.ptc
        w2, gb, gng = J.gw2, J.gb, self.gng

        def step(TS, S, Sb, s0, nts, t, h, d):
            (lrT, qT, kT, ktok, vtok, la, e1, e2, dcol, tb_, est, qin, kin, kst, vb, at, ost, opv, osq, rstd, rT, t1, brs) = TS
            INC = 2 if d == 0 else 3
            a = s0 + t * 128
            ti = a // 128
            k.dma(k.SP, lrT[:], PF[pfi["g_lf"] + d, 0:16, a:a + 128], lrT, [PF])
            k.dma(k.SP, qT[:], PF[pfi["g_q"] + h, :, a:a + 128], qT, [PF])
            k.dma(k.SP, kT[:], PF[pfi["g_k"] + h, :, a:a + 128], kT, [PF])
            k.dma(k.SP, ktok[:], PT[:, ti, ptc["g_k"] + h * 128:ptc["g_k"] + (h + 1) * 128], ktok, [PT])
            k.dma(k.SP, vtok[:], PT[:, ti, ptc["g_v"] + h * 256:ptc["g_v"] + (h + 1) * 256], vtok, [PT])
            pz = k.psum()
            k.mm(pz[:, 0:128], lrT[:, :], w2[0:16, d, h * 128:(h + 1) * 128], True, False, [lrT, w2], [pz], inc=False)
            k.mm(pz[:, 0:128], self.ones1[0:1, 0:128], gb[0:1, d, h * 128:(h + 1) * 128], False, True, [self.ones1, gb], [pz])
            k.act(la[:], pz[:, 0:128], AF.Exp, [pz], [la], scale=-1.0)
            k.act(la[:], la[:], AF.Ln, [la], [la], bias=1.0)
            pc = k.psum()
            k.mm(pc[:, 0:128], la[:], cst[:, INC, :], True, True, [la, cst], [pc], inc=False)
            k.mm(pc[:, 128:256], la[:], cst[:, 1, :], True, True, [la, cst], [pc], inc=False)
            k.mm(pc[:, 256:384], cst[:, INC, :], la[:], True, True, [la, cst], [pc], inc=False)
            k.mm(pc[:, 384:512], cst[:, 1, :], la[:], True, True, [la, cst], [pc])
            k.act(e1[:], pc[:, 0:128], AF.Exp, [pc], [e1], scale=-1.0 / 16)
            k.act(e2[:], pc[:, 0:128], AF.Exp, [pc], [e2], scale=1.0 / 16)
            k.act(dcol[:], pc[:, 128:129], AF.Exp, [pc], [dcol], scale=-1.0 / 16)
            k.act(tb_[:], pc[:, 256:384], AF.Copy, [pc], [tb_])
            k.ve(lambda: nc.vector.tensor_tensor(out=tb_[:], in0=pc[:, 384:512], in1=tb_[:], op=ALU.subtract), [pc, tb_], [tb_])
            k.act(est[:], tb_[:], AF.Exp, [tb_], [est], scale=-1.0 / 16)
            k.ve(lambda: nc.vector.scalar_tensor_tensor(out=qin[:], in0=qT[:], scalar=128.0 ** -0.5, in1=e1[:], op0=ALU.mult,
                                                        op1=ALU.mult), [qT, e1], [qin])
            k.ve(lambda: nc.vector.tensor_tensor(out=kin[:], in0=kT[:], in1=e2[:], op=ALU.mult), [kT, e2], [kin])
            k.ve(lambda: nc.vector.tensor_tensor(out=kst[:], in0=ktok[:], in1=est[:], op=ALU.mult), [ktok, est], [kst])
            k.act(vb[:], vtok[:], AF.Copy, [vtok], [vb])
            pa = k.psum()
            k.mm(pa[:, 0:128], kin[:], qin[:], True, True, [kin, qin], [pa])
            k.ve(lambda: nc.vector.tensor_tensor(out=at[:], in0=pa[:, 0:128], in1=cst[:, INC, :], op=ALU.mult), [pa, cst], [at])
            po = k.psum()
            for c in range(2):
                k.mm(po[:, c * 128:(c + 1) * 128], vb[:, c * 128:(c + 1) * 128], at[:], True, False, [vb, at], [po], inc=False)
                k.mm(po[:, c * 128:(c + 1) * 128], Sb[:, c * 128:(c + 1) * 128], qin[:], False, True, [Sb, qin], [po], inc=(c == 1))
            oa = J.OA[2 * h:2 * h + 2, :, a:a + 128].rearrange("c p t -> p c t")
            pov = po[:, 0:256].rearrange("p (c t) -> p c t", c=2)
            if d == 0:
                k.act(ost[:], pov, AF.Copy, [po], [ost])
                k.dma(k.POOL, oa, ost[:], J.OA, [ost], part=True)
            else:
                k.dma(k.SP, opv[:], oa, opv, [J.OA])
                k.ve(lambda: nc.vector.tensor_tensor(out=ost[:], in0=pov, in1=opv[:], op=ALU.add), [po, opv], [ost])
                k.act(osq[:], ost[:], AF.Square, [ost], [osq])
                pr = k.psum()
                k.mm(pr[:, 0:128], cst[:, 1, :], osq[:, 0, :], True, False, [cst, osq], [pr], inc=False)
                k.mm(pr[:, 0:128], cst[:, 1, :], osq[:, 1, :], False, True, [cst, osq], [pr])
                k.act(rstd[:], pr[:, 0:128], AF.Sqrt, [pr], [rstd], bias=EPS, scale=1.0 / 256)
                k.ve(lambda: nc.vector.reciprocal(out=rstd[:], in_=rstd[:]), [rstd], [rstd])
                k.dma(k.SP, rT[:], PF[pfi["g_r"] + 2 * h:pfi["g_r"] + 2 * h + 2, :, a:a + 128].rearrange("c p t -> p c t"), rT, [PF])
                k.act(rT[:], rT[:], AF.Silu, [rT], [rT])
                for c in range(2):
                    k.ve(lambda: nc.vector.scalar_tensor_tensor(out=t1[:], in0=ost[:, c, :], scalar=gng[:, l, c:c + 1], in1=rstd[:],
                                                                op0=ALU.mult, op1=ALU.mult), [ost, gng, rstd], [t1])
                    k.ve(lambda: nc.vector.tensor_tensor(out=brs[:, c, :], in0=t1[:], in1=rT[:, c, :], op=ALU.mult), [t1, rT], [brs])
                k.dma(k.POOL, J.BR[2 * h:2 * h + 2, :, a:a + 128].rearrange("c p t -> p c t"), brs[:], J.BR, [brs], part=True)
            pS = k.psum()
            k.mm(pS[:, 0:256], kst[:], vb[:], True, True, [kst, vb], [pS])
            k.ve(lambda: nc.vector.scalar_tensor_tensor(out=S[:], in0=S[:], scalar=dcol[:, 0:1], in1=pS[:, 0:256], op0=ALU.mult,
                                                        op1=ALU.add), [S, dcol, pS], [S])
            k.act(Sb[:], S[:], AF.Copy, [S], [Sb])

        for (s0, nts) in J.seqs:
            for d in range(2):
                for h in range(J.nG):
                    if J.kind == 'p':
                        k.ve(lambda: nc.vector.memset(S_l[h][:], 0.0), [], [S_l[h]])
                    else:
                        k.dma(k.SP, S_l[h][:], self.dr["sg_in"].t[l, d], S_l[h])
                    k.act(Sb_l[h][:], S_l[h][:], AF.Copy, [S_l[h]], [Sb_l[h]])
                order = list(range(nts)) if d == 0 else list(range(nts - 1, -1, -1))
                for t in order:
                    for h in range(J.nG):
                        step(sets[h % NSET], S_l[h], Sb_l[h], s0, nts, t, h, d)
                if J.kind == 'p':
                    so = self.dr["o_gf" if d == 0 else "o_gb"]
                    for h in range(J.nG):
                        k.dma(k.POOL, so[l, s0 // 256, h], S_l[h][:], so, [S_l[h]], part=True)

    def dn(self, J, l, st_):
        k, nc = self.k, self.nc
        cst, cstb = self.cst, self.cstb
        NSET = min(3, J.nD)
        sets = []
        for si_ in range(NSET):
            sb = lambda n, sh, dt, dma=False, si_=si_: k.sb("dn%d_" % si_ + n, sh, dt, dma=dma, stack=st_,
                                                            semg=(k.semg("dnset%d" % si_) if dma else None))
            xq = sb("xq", [128, 132], F32, True); xk = sb("xk", [128, 132], F32, True); xv = sb("xv", [128, 132], F32, True)
            cq = sb("cq", [128, 128], F32); ck = sb("ck", [128, 128], F32); cv_ = sb("cv", [128, 128], F32)
            sq2 = sb("sq2", [128, 256], F32); rn = sb("rn", [128, 256], F32)
            qn = sb("qn", [128, 128], F32); kn = sb("kn", [128, 128], F32); qnb = sb("qnb", [128, 128], BF16); knb = sb("knb", [128, 128], BF16)
            ktv = sb("ktv", [128, 256], F32)
            ab = sb("ab", [128, 4 * J.nD], F32, True)
            ea = sb("ea", [128, 1], F32); gcl = sb("g", [128, 1], F32); beta = sb("be", [128, 1], F32); nbeta = sb("nbe", [128, 1], F32)
            Dg = sb("Dg", [128, 128], F32); Db = sb("Db", [128, 128], F32)
            gcs = sb("gcs", [128, 130], F32); brow = sb("brow", [128, 128], F32); ngc = sb("ngc", [128, 1], F32)
            tm1 = sb("tm1", [128, 128], F32); DTi = sb("DTi", [128, 128], F32); tm2 = sb("tm2", [128, 128], F32); Di = sb("Di", [128, 128], F32)
            YX = sb("YX", [128, 256], F32)
            EXF = sb("EXF", [128, 2, 7, 128], F32)
            DR = [sb("DR%d" % j, [128, 256], F32) for j in range(2)]
            WZ = sb("WZ", [128, 256], F32)
            Rb = sb("Rb", [128, 128], BF16)
            attn = sb("attn", [128, 128], BF16)
            eg = sb("eg", [128, 1], F32); bg = sb("bg", [128, 1], F32); kstc = sb("kstc", [128, 1], F32); egl = sb("egl", [128, 1], F32)
            vbt = sb("vbt", [128, 128], BF16); kbe = sb("kbe", [128, 128], BF16); kst = sb("kst", [128, 128], BF16)
            erow = sb("erow", [128, 128], F32); qdec = sb("qdec", [128, 128], BF16)
            nwT = sb("nwT", [128, 128], BF16); vnew = sb("vnew", [128, 128], BF16)
            ost = sb("os", [128, 128], F32); opv = sb("op", [128, 128], F32, True); osq = sb("oq", [128, 128], F32)
            rstd = sb("rs", [128, 128], F32); gT = sb("gT", [128, 128], F32, True); t1 = sb("t1", [128, 128], F32); brs = sb("br", [128, 128], BF16)
            sets.append((xq, xk, xv, cq, ck, cv_, sq2, rn, qn, kn, qnb, knb, ktv, ab, ea, gcl, beta, nbeta, Dg, Db, gcs, brow, ngc, tm1, DTi, tm2, Di, YX, EXF, DR, WZ, Rb, attn, eg, bg, kstc, egl, vbt, kbe, kst, erow, qdec, nwT, vnew, ost, opv, osq, rstd, gT, t1, brs))
        sb = lambda n, sh, dt, dma=False: k.sb("dn_" + n, sh, dt, dma=dma, stack=st_, semg=(k.semg("dnSS") if dma else None))
        S_l = [sb("S%d" % h_, [128, 128], F32, True) for h_ in range(J.nD)]
        Sb_l = [sb("Sb%d" % h_, [128, 128], BF16) for h_ in range(J.nD)]
        PF, PT, pfi, ptc, nD = J.PF, J.PT, J.pfi, J.ptc, J.nD
        cw, dtb, nA, dng = J.cw, J.dtb, J.nA, self.dng

        def step(TS, S, Sb, s0, nts, t, h, d):
            (xq, xk, xv, cq, ck, cv_, sq2, rn, qn, kn, qnb, knb, ktv, ab, ea, gcl, beta, nbeta, Dg, Db, gcs, brow, ngc, tm1, DTi, tm2, Di, YX, EXF, DR, WZ, Rb, attn, eg, bg, kstc, egl, vbt, kbe, kst, erow, qdec, nwT, vnew, ost, opv, osq, rstd, gT, t1, brs) = TS
            INC, STR, NEGI = (2, 4, 6) if d == 0 else (3, 5, 7)
            STRT, NEGIT = (5, 7) if d == 0 else (4, 6)
            a = s0 + t * 128
            ti = a // 128
            lo = max(s0, a - 2)
            hi = min(s0 + nts * 128, a + 130)
            for (x_, nm, blk) in ((xq, "d_q", h), (xk, "d_k", nD + h), (xv, "d_v", 2 * nD + h)):
                if lo > a - 2 or hi < a + 130:
                    k.ve(lambda: nc.vector.memset(x_[:], 0.0), [], [x_])
                k.dma(k.SP, x_[:, lo - (a - 2):hi - (a - 2)], PF[pfi[nm] + h, :, lo:hi], x_, [PF], part=True)
            for (x_, c_, blk) in ((xq, cq, h), (xk, ck, nD + h), (xv, cv_, 2 * nD + h)):
                k.ve(lambda: nc.vector.tensor_scalar_mul(out=c_[:], in0=x_[:, 0:128], scalar1=cw[:, blk, 0:1]), [x_, cw], [c_])
                for tp in range(1, 5):
                    k.ve(lambda: nc.vector.scalar_tensor_tensor(out=c_[:], in0=x_[:, tp:tp + 128], scalar=cw[:, blk, tp:tp + 1], in1=c_[:],
                                                                op0=ALU.mult, op1=ALU.add), [x_, cw, c_], [c_])
                k.act(c_[:], c_[:], AF.Silu, [c_], [c_])
            k.act(sq2[:, 0:128], cq[:], AF.Square, [cq], [sq2])
            k.act(sq2[:, 128:256], ck[:], AF.Square, [ck], [sq2])
            pn = k.psum()
            k.mm(pn[:, 0:256], cst[:, 1, :], sq2[:], True, True, [cst, sq2], [pn])
            k.act(rn[:], pn[:, 0:256], AF.Sqrt, [pn], [rn], bias=EPS, scale=1.0)
            k.ve(lambda: nc.vector.reciprocal(out=rn[:], in_=rn[:]), [rn], [rn])
            k.ve(lambda: nc.vector.scalar_tensor_tensor(out=qn[:], in0=cq[:], scalar=128.0 ** -0.5, in1=rn[:, 0:128], op0=ALU.mult,
                                                        op1=ALU.mult), [cq, rn], [qn])
            k.ve(lambda: nc.vector.tensor_tensor(out=kn[:], in0=ck[:], in1=rn[:, 128:256], op=ALU.mult), [ck, rn], [kn])
            k.act(qnb[:], qn[:], AF.Copy, [qn], [qnb])
            k.act(knb[:], kn[:], AF.Copy, [kn], [knb])
            pt_ = k.psum()
            k.op(k.PE, lambda: nc.tensor.transpose(pt_[:, 0:128], kn[:], cst[:, 0, :]), [kn, cst], [pt_], inc=False)
            k.op(k.PE, lambda: nc.tensor.transpose(pt_[:, 128:256], cv_[:], cst[:, 0, :]), [cv_, cst], [pt_])
            k.act(ktv[:], pt_[:, 0:256], AF.Copy, [pt_], [ktv])
            k.dma(k.SP, ab[:], PT[:, ti, ptc["d_ab"]:ptc["d_ab"] + 4 * nD], ab, [PT])
            k.act(ea[:], ab[:, d * nD + h:d * nD + h + 1], AF.Exp, [ab, dtb], [ea], bias=dtb[:, d, h:h + 1], scale=1.0)
            k.act(ea[:], ea[:], AF.Ln, [ea], [ea], bias=1.0)
            k.ve(lambda: nc.vector.tensor_tensor(out=gcl[:], in0=ea[:], in1=nA[:, d, h:h + 1], op=ALU.mult), [ea, nA], [gcl])
            k.act(beta[:], ab[:, 2 * nD + d * nD + h:2 * nD + d * nD + h + 1], AF.Sigmoid, [ab], [beta])
            k.ve(lambda: nc.vector.tensor_scalar_mul(out=nbeta[:], in0=beta[:], scalar1=-1.0), [beta], [nbeta])
            k.ve(lambda: nc.vector.tensor_scalar_mul(out=Dg[:], in0=cst[:, INC, :], scalar1=gcl[:, 0:1]), [cst, gcl], [Dg])
            k.ve(lambda: nc.vector.tensor_scalar_mul(out=Db[:], in0=cst[:, 0, :], scalar1=beta[:, 0:1]), [cst, beta], [Db])
            pgc = k.psum()
            k.mm(pgc[:, 0:128], cst[:, 1, :], Dg[:], True, True, [cst, Dg], [pgc], inc=False)
            k.mm(pgc[:, 128:129], cst[:, INC, :], gcl[:], True, True, [cst, gcl], [pgc], inc=False)
            k.mm(pgc[:, 129:130], cst[:, 1, :], gcl[:], True, True, [cst, gcl], [pgc], inc=False)
            k.mm(pgc[:, 256:384], cst[:, 1, :], Db[:], True, True, [cst, Db], [pgc])
            k.act(gcs[:], pgc[:, 0:130], AF.Copy, [pgc], [gcs])
            k.act(brow[:], pgc[:, 256:384], AF.Copy, [pgc], [brow])
            k.ve(lambda: nc.vector.tensor_scalar_mul(out=ngc[:], in0=gcs[:, 128:129], scalar1=-1.0), [gcs], [ngc])
            k.ve(lambda: nc.vector.tensor_tensor(out=tm1[:], in0=gcs[:, 0:128], in1=cst[:, NEGI, :], op=ALU.add), [gcs, cst], [tm1])
            k.act(DTi[:], tm1[:], AF.Exp, [tm1, ngc], [DTi], bias=ngc[:, 0:1], scale=1.0)
            k.ve(lambda: nc.vector.scalar_tensor_tensor(out=tm2[:], in0=gcs[:, 0:128], scalar=-1.0, in1=cst[:, NEGIT, :], op0=ALU.mult,
                                                        op1=ALU.add), [gcs, cst], [tm2])
            k.act(Di[:], tm2[:], AF.Exp, [tm2, gcs], [Di], bias=gcs[:, 128:129], scale=1.0)
            pg = k.psum()
            k.mm(pg[:, 0:128], knb[:], knb[:], True, True, [knb], [pg], inc=False)
            k.mm(pg[:, 128:256], knb[:], qnb[:], True, True, [knb, qnb], [pg])
            k.ve(lambda: nc.vector.tensor_tensor(out=tm1[:], in0=DTi[:], in1=cst[:, STR, :], op=ALU.mult), [DTi, cst], [tm1])
            k.ve(lambda: nc.vector.tensor_tensor(out=tm1[:], in0=tm1[:], in1=brow[:], op=ALU.mult), [tm1, brow], [tm1])
            k.ve(lambda: nc.vector.scalar_tensor_tensor(out=YX[:, 0:128], in0=pg[:, 0:128], scalar=-1.0, in1=tm1[:], op0=ALU.mult,
                                                        op1=ALU.mult), [pg, tm1], [YX])
            k.ve(lambda: nc.vector.tensor_tensor(out=tm2[:], in0=Di[:], in1=cst[:, STRT, :], op=ALU.mult), [Di, cst], [tm2])
            k.ve(lambda: nc.vector.scalar_tensor_tensor(out=YX[:, 128:256], in0=pg[:, 0:128], scalar=nbeta[:, 0:1], in1=tm2[:],
                                                        op0=ALU.mult, op1=ALU.mult), [pg, nbeta, tm2], [YX])
            k.ve(lambda: nc.vector.tensor_tensor(out=attn[:], in0=pg[:, 128:256], in1=DTi[:], op=ALU.mult), [pg, DTi], [attn])
            bm = self.bmask
            for hf in range(2):
                k.ve(lambda: nc.vector.tensor_tensor(out=EXF[:, hf, :, :], in0=bm[:],
                                                     in1=YX[:, hf * 128:(hf + 1) * 128].unsqueeze(1).broadcast_to([128, 7, 128]),
                                                     op=ALU.mult), [bm, YX], [EXF])
            rc = 0
            k.act(DR[0][:, 0:128], cst[:, 0, :], AF.Copy, [cst], [DR[0]])
            k.act(DR[0][:, 128:256], cst[:, 0, :], AF.Copy, [cst], [DR[0]])
            for lv in range(7):
                Dc = DR[rc]
                pp = k.psum()
                k.mm(pp[:, 0:128], EXF[:, 0, lv, :], Dc[:, 0:128], True, True, [EXF, Dc], [pp], inc=False)
                k.mm(pp[:, 128:256], EXF[:, 1, lv, :], Dc[:, 128:256], True, True, [EXF, Dc], [pp])
                k.act(WZ[:], pp[:, 0:256], AF.Copy, [pp], [WZ])
                pz = k.psum()
                k.mm(pz[:, 0:128], Dc[:, 128:256], WZ[:, 0:128], True, True, [Dc, WZ], [pz], inc=False)
                k.mm(pz[:, 128:256], Dc[:, 0:128], WZ[:, 128:256], True, True, [Dc, WZ], [pz])
                k.ve(lambda: nc.vector.tensor_tensor(out=DR[1 - rc][:], in0=pz[:, 0:256], in1=Dc[:], op=ALU.add), [pz, Dc], [DR[1 - rc]])
                rc = 1 - rc
            k.act(Rb[:], DR[rc][:, 128:256], AF.Copy, [DR[rc]], [Rb])
            R = Rb
            k.act(eg[:], gcs[:, 128:129], AF.Exp, [gcs], [eg])
            k.ve(lambda: nc.vector.tensor_tensor(out=bg[:], in0=beta[:], in1=eg[:], op=ALU.mult), [beta, eg], [bg])
            k.ve(lambda: nc.vector.tensor_scalar_mul(out=vbt[:], in0=ktv[:, 128:256], scalar1=beta[:, 0:1]), [ktv, beta], [vbt])
            k.ve(lambda: nc.vector.tensor_scalar_mul(out=kbe[:], in0=ktv[:, 0:128], scalar1=bg[:, 0:1]), [ktv, bg], [kbe])
            k.act(kstc[:], gcs[:, 128:129], AF.Exp, [gcs], [kstc], bias=gcs[:, 129:130], scale=-1.0)
            k.ve(lambda: nc.vector.tensor_scalar_mul(out=kst[:], in0=ktv[:, 0:128], scalar1=kstc[:, 0:1]), [ktv, kstc], [kst])
            k.act(egl[:], gcs[:, 129:130], AF.Exp, [gcs], [egl])
            k.act(erow[:], gcs[:, 0:128], AF.Exp, [gcs], [erow])
            k.ve(lambda: nc.vector.tensor_tensor(out=qdec[:], in0=qn[:], in1=erow[:], op=ALU.mult), [qn, erow], [qdec])
            pw = k.psum()
            k.mm(pw[:, 0:128], kbe[:], R[:], True, True, [kbe, R], [pw])
            k.act(nwT[:], pw[:, 0:128], AF.Identity, [pw], [nwT], scale=-1.0)
            pv = k.psum()
            k.mm(pv[:, 0:128], R[:], vbt[:], True, False, [R, vbt], [pv], inc=False)
            k.mm(pv[:, 0:128], nwT[:], Sb[:], False, True, [nwT, Sb], [pv])
            k.act(vnew[:], pv[:, 0:128], AF.Copy, [pv], [vnew])
            po = k.psum()
            k.mm(po[:, 0:128], Sb[:], qdec[:], True, False, [Sb, qdec], [po], inc=False)
            k.mm(po[:, 0:128], vnew[:], attn[:], False, True, [vnew, attn], [po])
            oi = 2 * J.nG + h
            if d == 0:
                k.act(ost[:], po[:, 0:128], AF.Copy, [po], [ost])
                k.dma(k.POOL, J.OA[oi, :, a:a + 128], ost[:], J.OA, [ost], part=True)
            else:
                k.dma(k.SP, opv[:], J.OA[oi, :, a:a + 128], opv, [J.OA])
                k.ve(lambda: nc.vector.tensor_tensor(out=ost[:], in0=po[:, 0:128], in1=opv[:], op=ALU.add), [po, opv], [ost])
                k.act(osq[:], ost[:], AF.Square, [ost], [osq])
                pr = k.psum()
                k.mm(pr[:, 0:128], cst[:, 1, :], osq[:], True, True, [cst, osq], [pr])
                k.act(rstd[:], pr[:, 0:128], AF.Sqrt, [pr], [rstd], bias=EPS, scale=1.0 / 128)
                k.ve(lambda: nc.vector.reciprocal(out=rstd[:], in_=rstd[:]), [rstd], [rstd])
                k.dma(k.SP, gT[:], PF[pfi["d_g"] + h, :, a:a + 128], gT, [PF])
                k.act(gT[:], gT[:], AF.Silu, [gT], [gT])
                k.ve(lambda: nc.vector.scalar_tensor_tensor(out=t1[:], in0=ost[:], scalar=dng[:, l:l + 1], in1=rstd[:], op0=ALU.mult,
                                                            op1=ALU.mult), [ost, dng, rstd], [t1])
                k.ve(lambda: nc.vector.tensor_tensor(out=brs[:], in0=t1[:], in1=gT[:], op=ALU.mult), [t1, gT], [brs])
                k.dma(k.POOL, J.BR[J.nbr + h, :, a:a + 128], brs[:], J.BR, [brs], part=True)
            pS = k.psum()
            k.mm(pS[:, 0:128], kst[:], vnew[:], True, True, [kst, vnew], [pS])
            k.ve(lambda: nc.vector.scalar_tensor_tensor(out=S[:], in0=S[:], scalar=egl[:, 0:1], in1=pS[:, 0:128], op0=ALU.mult,
                                                        op1=ALU.add), [S, egl, pS], [S])
            k.act(Sb[:], S[:], AF.Copy, [S], [Sb])

        for (s0, nts) in J.seqs:
            for d in range(2):
                for h in range(nD):
                    if J.kind == 'p':
                        k.ve(lambda: nc.vector.memset(S_l[h][:], 0.0), [], [S_l[h]])
                    else:
                        k.dma(k.SP, S_l[h][:], self.dr["sd_in"].t[l, d, h], S_l[h])
                    k.act(Sb_l[h][:], S_l[h][:], AF.Copy, [S_l[h]], [Sb_l[h]])
                order = list(range(nts)) if d == 0 else list(range(nts - 1, -1, -1))
                for t in order:
                    for h in range(nD):
                        step(sets[h % NSET], S_l[h], Sb_l[h], s0, nts, t, h, d)
                if J.kind == 'p':
                    so = self.dr["o_df" if d == 0 else "o_db"]
                    for h in range(nD):
                        k.dma(k.POOL, so[l, s0 // 256, h], S_l[h][:], so, [S_l[h]], part=True)

    def swa(self, J, l, st_):
        k, nc = self.k, self.nc
        cst, cstb = self.cst, self.cstb
        sb = lambda n, sh, dt, dma=False: k.sb("sw_" + n, sh, dt, dma=dma, stack=st_)
        PF, PT, pfi, ptc = J.PF, J.PT, J.pfi, J.ptc
        L = J.seqs[0][1] * 128
        nt = L // 128
        ld = sb("ld", [128, 128], F32, True); cs = sb("cs", [128, 128], F32, True); sn = sb("sn", [128, 128], F32, True)
        rq = sb("rq", [128, 128], F32)
        kb = sb("kb", [128, L], BF16); vb = sb("vb", [128, nt, 128], BF16, True)
        qb = sb("qb", [128, 128], BF16); pT = [sb("pT%d" % j, [128, 128], BF16) for j in range(2)]
        den = sb("den", [128, 128], F32); ob = sb("ob", [128, 128], BF16)
        lat = (J.kind == 's')
        if lat:
            kc = sb("kc", [128, 512], BF16, True); vc = sb("vc", [128, 4, 128], BF16, True)
            k.dma(k.POOL, kc[:], self.dr["kc_in"].t[l], kc)
            k.dma(k.POOL, vc[:], self.dr["vc_in"].t[l], vc)

        def rope(dst, dst_t, a):
            k.dma(k.SP, cs[:], self.dr["cos"].t[:, a:a + 128], cs)
            k.dma(k.SP, sn[:], self.dr["sin"].t[:, a:a + 128], sn)
            pr = k.psum()
            k.mm(pr[:, 0:128], cst[:, 8, :], ld[:], True, True, [cst, ld], [pr])
            k.ve(lambda: nc.vector.tensor_tensor(out=rq[:], in0=pr[:, 0:128], in1=sn[:], op=ALU.mult), [pr, sn], [rq])
            k.ve(lambda: nc.vector.tensor_tensor(out=cs[:], in0=ld[:], in1=cs[:], op=ALU.mult), [ld, cs], [cs])
            k.ve(lambda: nc.vector.tensor_tensor(out=dst, in0=cs[:], in1=rq[:], op=ALU.add), [cs, rq], [dst_t])

        for (s0, nts) in J.seqs:
            for kv in range(J.nKV):
                for t in range(nts):
                    a = s0 + t * 128
                    k.dma(k.SP, ld[:], PF[pfi["s_k"] + kv, :, a:a + 128], ld, [PF])
                    if lat:
                        rope(kb[:, t * 128:(t + 1) * 128], kb, a)
                    else:
                        k.act(kb[:, t * 128:(t + 1) * 128], ld[:], AF.Copy, [ld], [kb])
                    k.dma(k.POOL, vb[:, t, :], PT[:, a // 128, ptc["s_v"] + kv * 128:ptc["s_v"] + (kv + 1) * 128], vb, [PT], part=True)
                for hq in range(kv * (J.nQ // J.nKV), (kv + 1) * (J.nQ // J.nKV)):
                    for t in range(nts):
                        a = s0 + t * 128
                        k.dma(k.SP, ld[:], PF[pfi["s_q"] + hq, :, a:a + 128], ld, [PF])
                        if lat:
                            rope(qb[:], qb, a)
                            k.ve(lambda: nc.vector.tensor_scalar_mul(out=qb[:], in0=qb[:], scalar1=128.0 ** -0.5), [qb], [qb])
                        else:
                            k.act(qb[:], ld[:], AF.Identity, [ld], [qb], scale=128.0 ** -0.5)
                        if lat:
                            keys = [(kb[:, j * 128:(j + 1) * 128], vb[:, j, :], (3 if j < t else (2 if j > t else None)), kb, vb)
                                    for j in (t - 1, t, t + 1) if 0 <= j < nts]
                            keys += [(kc[:, j * 128:(j + 1) * 128], vc[:, j, :], None, kc, vc) for j in range(4)]
                        else:
                            keys = [(kb[:, j * 128:(j + 1) * 128], vb[:, j, :], None, kb, vb) for j in range(nts)]
                        k.ps_limit = 6
                        pso = k.ps[6]
                        psd = k.ps[7]
                        for ki, (kap, vap, msk, kt_, vt_) in enumerate(keys):
                            pss = k.psum()
                            k.mm(pss[:, 0:128], kap, qb[:], True, True, [kt_, qb], [pss])
                            p_ = pT[ki % 2]
                            k.act(p_[:], pss[:, 0:128], AF.Exp, [pss], [p_])
                            if msk is not None:
                                k.ve(lambda: nc.vector.tensor_tensor(out=p_[:], in0=p_[:], in1=cstb[:, msk, :], op=ALU.mult), [p_, cstb], [p_])
                            last = (ki == len(keys) - 1)
                            k.mm(pso[:, 0:128], vap, p_[:], ki == 0, last, [vt_, p_], [pso], inc=True)
                            k.mm(psd[:, 0:128], cstb[:, 1, :], p_[:], ki == 0, last, [cstb, p_], [psd], inc=True)
                        k.ve(lambda: nc.vector.tensor_scalar(out=den[:], in0=psd[:, 0:128], scalar1=J.esink[:, hq:hq + 1], scalar2=None, op0=ALU.add),
                             [psd, J.esink], [den])
                        k.ve(lambda: nc.vector.reciprocal(out=den[:], in_=den[:]), [den], [den])
                        k.ve(lambda: nc.vector.tensor_tensor(out=ob[:], in0=pso[:, 0:128], in1=den[:], op=ALU.mult), [pso, den], [ob])
                        k.dma(k.POOL, J.BR[2 * J.nbr + hq, :, a:a + 128], ob[:], J.BR, [ob], part=True)
        k.ps_limit = 8

    def phase_c(self, l):
        k, nc = self.k, self.nc
        win = self.dr["w_in"].t[l]
        wbr = self.dr["w_branch"].t[l]
        wo = self.dr["w_o"].t[l]
        mc = self.modc[l]
        with ExitStack() as st_:
            hb = k.sb("pc_h", [128, 16, 512], BF16, dma=True, stack=st_)
            brb = k.sb("pc_b", [128, 24, 512], BF16, dma=True, stack=st_)
            brq = k.sb("pc_q", [128, 24, 512], BF16, dma=True, stack=st_)
            mixT = k.sb("pc_m", [128, 16, 512], BF16, stack=st_)
            x_sb = k.sb("pc_x", [128, 16, 512], F32, dma=True, stack=st_)
            sg = k.sb("pc_sg", [128, 512], F32, stack=st_)
            acc = k.sb("pc_acc", [128, 512], F32, stack=st_)
            tmp = k.sb("pc_tmp", [128, 512], F32, stack=st_)
            for blk in range(TT // 512):
                t0 = blk * 512
                which = 0 if t0 < TP else 1
                if which == 1 and self.cfg.get("prompt_only", False):
                    continue
                k.dma(k.SP, hb[:, :, :], self.HT[:, :, t0:t0 + 512].rearrange("c p t -> p c t"), hb, [self.HT])
                if which == 0:
                    k.dma(k.SP, brb[:, :, :], self.dr["BR_p"][:, :, t0:t0 + 512].rearrange("c p t -> p c t"), brb, [self.dr["BR_p"]])
                else:
                    ts0 = t0 - TP
                    BG = self.BRG
                    for q in range(4):
                        for n in range(3):
                            for r in range(4):
                                src = BG[2 * n:2 * n + 2, r, :, q * 1024 + ts0:q * 1024 + ts0 + 512].rearrange("c p t -> p c t")
                                dst = brq[:, n * 8 + 2 * r:n * 8 + 2 * r + 2, :]
                                k.dma(k.SP, dst, src, brq, [BG], part=(n + r > 0))
                        if q == 0:
                            k.ve(lambda: nc.vector.tensor_scalar_mul(out=brb[:], in0=brq[:], scalar1=self.oh[:, 0:1]), [brq, self.oh], [brb])
                        else:
                            k.ve(lambda: nc.vector.scalar_tensor_tensor(out=brb[:], in0=brq[:], scalar=self.oh[:, q:q + 1], in1=brb[:], op0=ALU.mult,
                                                                        op1=ALU.add), [brq, self.oh, brb], [brb])
                for m in range(16):
                    s = self.next_wslot()
                    v = s[:, 0:16 * 384].rearrange("p (c n) -> p c n", c=16)
                    for n in range(3):
                        c0 = MG0 + n * D + m * 128
                        k.dma(k.POOL, v[:, :, n * 128:(n + 1) * 128], win[:, c0:c0 + 128].rearrange("(c p) n -> p c n", p=128), s, part=(n > 0))
                    s2 = self.next_oslot()
                    v2 = s2[:, 0:24 * 128].rearrange("p (c n) -> p c n", c=24)
                    for n in range(3):
                        k.dma(k.POOL, v2[:, n * 8:(n + 1) * 8, :], wbr[n, :, m * 128:(m + 1) * 128].rearrange("(c p) n -> p c n", p=128), s2, part=(n > 0))
                    for n in range(3):
                        pg = k.psum()
                        pu = k.psum()
                        for c in range(16):
                            k.mm(pg[:], v[:, c, n * 128:(n + 1) * 128], hb[:, c, :], c == 0, c == 15, [s, hb], [pg])
                        for c in range(8):
                            k.mm(pu[:], v2[:, n * 8 + c, :], brb[:, n * 8 + c, :], c == 0, c == 7, [s2, brb], [pu])
                        k.act(sg[:], pg[:], AF.Sigmoid, [pg], [sg])
                        if n == 0:
                            k.ve(lambda: nc.vector.tensor_tensor(out=acc[:], in0=sg[:], in1=pu[:], op=ALU.mult), [sg, pu], [acc])
                        else:
                            k.ve(lambda: nc.vector.tensor_tensor(out=tmp[:], in0=sg[:], in1=pu[:], op=ALU.mult), [sg, pu], [tmp])
                            k.ve(lambda: nc.vector.tensor_tensor(out=acc[:], in0=acc[:], in1=tmp[:], op=ALU.add), [acc, tmp], [acc])
                    k.act(mixT[:, m, :], acc[:], AF.Copy, [acc], [mixT])
                k.dma(k.SP, x_sb[:, :, :], self.xs[:, :, t0:t0 + 512].rearrange("c p t -> p c t"), x_sb, [self.xs])
                for m in range(16):
                    s, v = self.load_w(wo, m * 128, 128)
                    py = k.psum()
                    for c in range(16):
                        k.mm(py[:], v[:, c, :], mixT[:, c, :], c == 0, c == 15, [s, mixT], [py])
                    gt = mc[:, 5 * 16 + m, which:which + 1]
                    k.ve(lambda: nc.vector.scalar_tensor_tensor(out=x_sb[:, m, :], in0=py[:], scalar=gt, in1=x_sb[:, m, :], op0=ALU.mult, op1=ALU.add),
                         [py, mc, x_sb], [x_sb])
                k.dma(k.SP, self.xs[:, :, t0:t0 + 512].rearrange("c p t -> p c t"), x_sb[:, :, :], self.xs, [x_sb], part=True)
            k.barrier()

    def setup_mixer(self):
        k, nc = self.k, self.nc
        self.HT = self.dint("HT", [16, 128, TT], BF16)
        self.HSRC = self.dint("HSRC", [16 * 128, TS], BF16)
        self.HGd = self.dint("HGd", [4 * 16 * 128, TS], BF16)
        self.HG = T(self.HGd.t.rearrange("(q r c p) t -> q r c p t", q=4, r=4, c=4), self.HGd.buf)
        self.BRSd = self.dint("BRSd", [6 * 128, LS], BF16)
        self.dr["BRS"] = T(self.BRSd.t.rearrange("(c p) t -> c p t", c=6), self.BRSd.buf)
        self.BRGd = self.dint("BRGd", [4 * 6 * 128, LS], BF16)
        self.BRG = T(self.BRGd.t.rearrange("(c r p) t -> c r p t", c=6, r=4), self.BRGd.buf)
        self.gng = k.sb("gng_sb", [128, NL, 2], F32, dma=True)
        k.dma(k.SP, self.gng[:], self.din("gngT", [128, NL, 2])[:, :, :], self.gng)
        self.dng = k.sb("dng_sb", [128, NL], F32, dma=True)
        k.dma(k.SP, self.dng[:], self.din("dngT", [128, NL])[:, :], self.dng)
        self.oh = k.sb("oh_sb", [128, 4], F32, dma=True)
        k.dma(k.SP, self.oh[:], self.din("oh", [128, 4])[:, :], self.oh)
        for kind, nG, nD, nQ in (("p", 4, 8, 8), ("s", 1, 2, 2)):
            self.din("gw2_" + kind, [NL, 16, 2, nG * 128])
            self.din("gb_" + kind, [NL, 1, 2, nG * 128])
            self.din("cw_" + kind, [NL, 128, 3 * nD, 5])
            self.din("dtb_" + kind, [NL, 1, 2 * nD])
            self.din("dal_" + kind, [NL, 1, 2 * nD])
            self.din("snk_" + kind, [NL, 1, nQ])
        self.din("w_in_s", [NL, D, mixer_cols(1, 2, 2, 1)["_tot"]])
        self.din("sg_in", [NL, 2, 128, 256])
        self.din("sd_in", [NL, 2, 2, 128, 128])
        self.din("kc_in", [NL, 128, 512])
        self.din("vc_in", [NL, 128, 4, 128])
        self.din("cos", [128, LS])
        self.din("sin", [128, LS])
        self.dout("o_k", [NL, TP, 256])
        self.dout("o_v", [NL, TP, 256])
        self.dout("o_gf", [NL, 4, 4, 128, 256])
        self.dout("o_gb", [NL, 4, 4, 128, 256])
        self.dout("o_df", [NL, 4, 8, 128, 128])
        self.dout("o_db", [NL, 4, 8, 128, 128])

    def collective(self, src, dst, rows_per):
        k, nc = self.k, self.nc
        E = k.POOL
        if src.buf.w is not None:
            k.wait(E, src.buf.w)
        for ev in list(dst.buf.r.values()):
            k.wait(E, ev)
        g = dst.buf.semg
        rows = src.th.ap().shape[0]
        for q in range(rows // rows_per):
            nc.gpsimd.collective_compute("AllGather", ALU.bypass, replica_groups=[[0, 1, 2, 3], [4, 5, 6, 7]],
                                         ins=[src.th.ap()[q * rows_per:(q + 1) * rows_per, :]],
                                         outs=[dst.th.ap()[q * 4 * rows_per:(q + 1) * 4 * rows_per, :]]).then_inc(g.sem)
            g.cnt += 1
        ev = ('d', g)
        src.buf.r[g] = ev
        dst.buf.w = ev
        dst.buf.r = {}

    def job_consts(self, J, l, st_):
        k, nc = self.k, self.nc
        kd = J.kind
        nG, nD, nQ = J.nG, J.nD, J.nQ
        sb = lambda n, sh, dma=True: k.sb("jc_" + kd + n, sh, F32, dma=dma, stack=st_)
        J.gw2 = sb("w2", [16, 2, nG * 128]); k.dma(k.SP, J.gw2[:], self.dr["gw2_" + kd].t[l], J.gw2)
        J.gb = sb("gb", [1, 2, nG * 128]); k.dma(k.SP, J.gb[:], self.dr["gb_" + kd].t[l], J.gb)
        J.cw = sb("cw", [128, 3 * nD, 5]); k.dma(k.SP, J.cw[:], self.dr["cw_" + kd].t[l], J.cw)
        J.dtb = sb("dtb", [128, 2, nD])
        k.dma(k.SP, J.dtb[:].rearrange("p d h -> p (d h)"), self.dr["dtb_" + kd].t[l].broadcast_to([128, 2 * nD]), J.dtb)
        J.nA = sb("nA", [128, 2, nD])
        k.dma(k.SP, J.nA[:].rearrange("p d h -> p (d h)"), self.dr["dal_" + kd].t[l].broadcast_to([128, 2 * nD]), J.nA)
        k.act(J.nA[:], J.nA[:], AF.Exp, [J.nA], [J.nA])
        k.ve(lambda: nc.vector.tensor_scalar_mul(out=J.nA[:], in0=J.nA[:], scalar1=-1.0), [J.nA], [J.nA])
        J.esink = sb("es", [128, nQ])
        k.dma(k.SP, J.esink[:], self.dr["snk_" + kd].t[l].broadcast_to([128, nQ]), J.esink)
        k.act(J.esink[:], J.esink[:], AF.Exp, [J.esink], [J.esink])

    def mixer(self, l):
        k, nc = self.k, self.nc
        with ExitStack() as st_:
            x_sb = k.sb("mx", [128, 16, 512], F32, dma=True, stack=st_)
            hT = k.sb("mh", [128, 16, 512], BF16, stack=st_)
            st = {"sq": [k.sb("msq%d" % j, [128, 512], F32, stack=st_) for j in range(2)],
                  "rstd": k.sb("mrs", [128, 512], F32, stack=st_)}
            for blk in range(TT // 512):
                t0 = blk * 512
                which = 0 if t0 < TP else 1
                self.norm_block(l, 1, t0, 512, which, x_sb, hT, st)
                k.dma(k.SP, self.HT[:, :, t0:t0 + 512].rearrange("c p t -> p c t"), hT[:, :, :], self.HT, [hT], part=True)
                if which == 1:
                    k.dma(k.SP, self.HSRC[:, t0 - TP:t0 - TP + 512].rearrange("(c p) t -> p c t", p=128), hT[:, :, :], self.HSRC, [hT], part=True)
            k.barrier()
        po = self.cfg.get("prompt_only", False)
        if not po:
            self.collective(self.HSRC, self.HGd, 512)
        for kind in (("p",) if po else ("p", "s")):
            J = self.mk_job(kind, l)
            if kind == "p":
                self.proj(J, lambda tb: [(0, 16, self.HT[:, :, 0:TP].rearrange("c p t -> p c t"))])
                for nm in ("k", "v"):
                    o = self.dr["o_" + nm]
                    c0 = J.ptc["s_" + nm]
                    k.dma(k.SP, o[l].rearrange("(n p) c -> p n c", p=128), J.PT[:, :, c0:c0 + 256], o, [J.PT], part=True)
            else:
                self.proj(J, lambda tb: [(4 * q, 4 * q + 4, self.HG[q, tb].rearrange("c p t -> p c t")) for q in range(4)])
            with ExitStack() as st_:
                self.job_consts(J, l, st_)
                with ExitStack() as s2:
                    self.gla(J, l, s2)
                    k.barrier()
                with ExitStack() as s2:
                    self.dn(J, l, s2)
                    k.barrier()
                with ExitStack() as s2:
                    self.swa(J, l, s2)
                    k.barrier()
        if not po:
            self.collective(self.BRSd, self.BRGd, 128)
        self.phase_c(l)


def make_consts():
    p = np.arange(128)[:, None]
    f = np.arange(128)[None, :]
    c = np.zeros((128, 9, 128), np.float32)
    c[:, 0] = (p == f)
    c[:, 1] = 1.0
    c[:, 2] = (p <= f)
    c[:, 3] = (p >= f)
    c[:, 4] = (p < f)
    c[:, 5] = (p > f)
    c[:, 6] = ((p <= f) - 1.0) * 30000.0
    c[:, 7] = ((p >= f) - 1.0) * 30000.0
    R = np.zeros((128, 128), np.float32)
    for base in (0, 64):
        for i in range(32):
            R[base + i, base + 32 + i] = -1.0
            R[base + 32 + i, base + i] = 1.0
    c[:, 8] = R.T
    return c


def make_bmask():
    i = np.arange(128)[:, None]
    j = np.arange(128)[None, :]
    m = np.zeros((128, 7, 128), np.float32)
    for lv in range(7):
        b = 1 << lv
        m[:, lv, :] = ((i // (2 * b)) == (j // (2 * b))) & (((i % (2 * b)) >= b) != ((j % (2 * b)) >= b))
    return m


def build_nc(cfg):
    nc = bass.Bass("TRN2", target_bir_lowering=False)
    es = ExitStack()
    with es:
        P = Prog(nc, es, cfg)
        P.din("w_mod", [NL, D, 9 * D])
        P.din("ffn_in", [NL, 2, D, 2 * DFF])
        P.din("ffn_out", [NL, 2, DFF, D])
        xTd = P.din("xT", [16, 128, TT])
        P.dout("yT", [16, 128, TT])
        blk = es.enter_context(nc.Block())
        P.setup()
        if cfg.get("mixer", False):
            P.din("w_in", [NL, D, DIN])
            P.din("w_branch", [NL, 3, 1024, D])
            P.din("w_o", [NL, D, D])
            P.setup_mixer()
        k = P.k
        with ExitStack() as st_:
            tmp = k.sb("ldx", [128, 16, 512], F32, dma=True, stack=st_)
            for b in range(TT // 512):
                k.dma(k.SP, tmp[:, :, :], xTd[:, :, b * 512:(b + 1) * 512].rearrange("c p t -> p c t"), tmp)
                k.dma(k.SP, P.xs[:, :, b * 512:(b + 1) * 512].rearrange("c p t -> p c t"), tmp[:, :, :], P.xs, [tmp], part=True)
            k.barrier()
        for l in range(cfg["layers"]):
            P.mods(l)
            if cfg.get("ffn1", True):
                P.ffn(l, 0)
            if cfg.get("mixer", False):
                P.mixer(l)
            if cfg.get("ffn2", False):
                P.ffn(l, 1)
        P.final()
        k.barrier()
    return nc


FULL_CFG = {"layers": NL, "ffn1": True, "mixer": True, "ffn2": True}
_NC_CACHE = {}


def _wins_cols(p):
    r = lambda a, b: list(range(a, b))
    c = []
    c += r(128 * p, 128 * p + 128)
    c += r(512 + 128 * p, 512 + 128 * p + 128)
    c += r(1024 + 256 * p, 1024 + 256 * p + 256)
    c += r(2048 + 256 * p, 2048 + 256 * p + 256)
    c += r(3072, 3104)
    for base in (3104, 4128, 5152):
        c += r(base + 256 * p, base + 256 * p + 256)
    for base in (6176, 6184, 6192, 6200):
        c += r(base + 2 * p, base + 2 * p + 2)
    c += r(6208 + 256 * p, 6208 + 256 * p + 256)
    c += r(7232 + 256 * p, 7232 + 256 * p + 256)
    kv = p // 2
    c += r(8256 + 128 * kv, 8256 + 128 * kv + 128)
    c += r(8512 + 128 * kv, 8512 + 128 * kv + 128)
    return np.array(c)


def kernel(x_prompt, x_sample, cache_attn_k, cache_attn_v, state_gla_fwd, state_gla_bwd, state_dn_fwd, state_dn_bwd, c, c_ctx,
           w_mod, b_mod, norm_g, ffn_in, ffn_out, w_in, gla_w2, gla_b, gla_norm_g, dn_conv, dn_a_log, dn_dt_bias, dn_norm_g,
           swa_sink, w_branch, w_o, final_norm_g):
    f = lambda a: np.ascontiguousarray(np.asarray(a, dtype=np.float32))
    (x_prompt, x_sample, cache_attn_k, cache_attn_v, state_gla_fwd, state_gla_bwd, state_dn_fwd, state_dn_bwd, c, c_ctx, w_mod, b_mod,
     norm_g, ffn_in, ffn_out, w_in, gla_w2, gla_b, gla_norm_g, dn_conv, dn_a_log, dn_dt_bias, dn_norm_g, swa_sink, w_branch, w_o,
     final_norm_g) = [f(a) for a in (x_prompt, x_sample, cache_attn_k, cache_attn_v, state_gla_fwd, state_gla_bwd, state_dn_fwd,
                                     state_dn_bwd, c, c_ctx, w_mod, b_mod, norm_g, ffn_in, ffn_out, w_in, gla_w2, gla_b, gla_norm_g,
                                     dn_conv, dn_a_log, dn_dt_bias, dn_norm_g, swa_sink, w_branch, w_o, final_norm_g)]
    in_maps = _prep(x_prompt, x_sample, cache_attn_k, cache_attn_v, state_gla_fwd, state_gla_bwd, state_dn_fwd, state_dn_bwd, c, c_ctx,
                    w_mod, b_mod, norm_g, ffn_in, ffn_out, w_in, gla_w2, gla_b, gla_norm_g, dn_conv, dn_a_log, dn_dt_bias, dn_norm_g,
                    swa_sink, w_branch, w_o, final_norm_g)
    if "nc" not in _NC_CACHE:
        _NC_CACHE["nc"] = build_nc(FULL_CFG)
    nc = _NC_CACHE["nc"]
    res = run_bass_kernel_spmd(nc, in_maps, core_ids=list(range(8)))
    return _post(res.results)


def _prep(x_prompt, x_sample, cache_attn_k, cache_attn_v, state_gla_fwd, state_gla_bwd, state_dn_fwd, state_dn_bwd, c, c_ctx,
          w_mod, b_mod, norm_g, ffn_in, ffn_out, w_in, gla_w2, gla_b, gla_norm_g, dn_conv, dn_a_log, dn_dt_bias, dn_norm_g,
          swa_sink, w_branch, w_o, final_norm_g):
    f = lambda a: np.ascontiguousarray(np.asarray(a, dtype=np.float32))
    tpos = np.arange(LS)
    inv = (10000.0 ** (-np.arange(32, dtype=np.float32) / 32)).astype(np.float32)
    pos = np.where(np.arange(128)[:, None] < 64, (tpos // 64)[None, :], (tpos % 64)[None, :]).astype(np.float32)
    ang = (pos * inv[np.arange(128) % 32][:, None]).astype(np.float32)
    cos_t, sin_t = f(np.cos(ang)), f(np.sin(ang))
    shared = {
        "cst": make_consts(), "bmask": make_bmask(), "w_mod": w_mod, "ffn_in": ffn_in, "ffn_out": ffn_out, "w_in": w_in, "w_branch": w_branch, "w_o": w_o,
        "bmT": f(b_mod.reshape(NL, 144, 128).transpose(2, 0, 1)),
        "ngT": f(np.concatenate([norm_g.reshape(6, D), final_norm_g[None]], 0).reshape(7, 16, 128).transpose(2, 0, 1)),
        "gngT": f(gla_norm_g.reshape(NL, 2, 128).transpose(2, 0, 1)), "dngT": f(dn_norm_g.T),
        "gw2_p": f(gla_w2.transpose(0, 2, 1, 3)), "gb_p": f(gla_b[:, None]),
        "cw_p": f(dn_conv.reshape(NL, 5, 24, 128).transpose(0, 3, 2, 1)),
        "dtb_p": f(dn_dt_bias.reshape(NL, 1, 16)), "dal_p": f(dn_a_log.reshape(NL, 1, 16)), "snk_p": f(swa_sink[:, None, :]),
        "cos": cos_t, "sin": sin_t,
    }
    in_maps = []
    for r in range(8):
        gi, p = r // 4, r % 4
        x = np.concatenate([x_prompt[4 * r:4 * r + 4].reshape(TP, D), x_sample[gi, TS * p:TS * (p + 1)]], 0)
        m = dict(shared)
        m["xT"] = f(x.T.reshape(16, 128, TT))
        m["cv"] = f(np.stack([c_ctx, c[gi]], -1).reshape(16, 128, 2).transpose(1, 0, 2))
        oh = np.zeros((128, 4), np.float32)
        oh[:, p] = 1.0
        m["oh"] = oh
        m["gw2_s"] = f(shared["gw2_p"][:, :, :, 128 * p:128 * p + 128])
        m["gb_s"] = f(shared["gb_p"][:, :, :, 128 * p:128 * p + 128])
        blocks = [2 * p, 2 * p + 1, 8 + 2 * p, 9 + 2 * p, 16 + 2 * p, 17 + 2 * p]
        m["cw_s"] = f(shared["cw_p"][:, :, blocks, :])
        m["dtb_s"] = f(dn_dt_bias[:, :, 2 * p:2 * p + 2].reshape(NL, 1, 4))
        m["dal_s"] = f(dn_a_log[:, :, 2 * p:2 * p + 2].reshape(NL, 1, 4))
        m["snk_s"] = f(swa_sink[:, None, 2 * p:2 * p + 2])
        m["w_in_s"] = f(w_in[:, :, _wins_cols(p)])
        m["sg_in"] = f(np.stack([state_gla_fwd[gi, :, p], state_gla_bwd[gi, :, p]], 1))
        m["sd_in"] = f(np.stack([state_dn_fwd[gi, :, 2 * p:2 * p + 2], state_dn_bwd[gi, :, 2 * p:2 * p + 2]], 1))
        m["kc_in"] = f(cache_attn_k[gi, :, :, p // 2, :].transpose(0, 2, 1))
        m["vc_in"] = f(cache_attn_v[gi, :, :, p // 2, :].reshape(NL, 4, 128, 128).transpose(0, 2, 1, 3))
        in_maps.append(m)
    return in_maps


def _post(results):
    B = 32
    y_prompt = np.zeros((B, 256, D), np.float32)
    y_sample = np.zeros((2, LS, D), np.float32)
    nk = np.zeros((B, NL, 256, 2, 128), np.float32)
    nv = np.zeros((B, NL, 256, 2, 128), np.float32)
    gf = np.zeros((B, NL, 4, 128, 256), np.float32)
    gb_ = np.zeros((B, NL, 4, 128, 256), np.float32)
    df = np.zeros((B, NL, 8, 128, 128), np.float32)
    db = np.zeros((B, NL, 8, 128, 128), np.float32)
    for r in range(8):
        gi, p = r // 4, r % 4
        o = results[r]
        y = np.asarray(o["yT"]).reshape(D, TT).T
        y_prompt[4 * r:4 * r + 4] = y[0:TP].reshape(4, 256, D)
        y_sample[gi, TS * p:TS * (p + 1)] = y[TP:]
        nk[4 * r:4 * r + 4] = np.asarray(o["o_k"]).reshape(NL, 4, 256, 2, 128).transpose(1, 0, 2, 3, 4)
        nv[4 * r:4 * r + 4] = np.asarray(o["o_v"]).reshape(NL, 4, 256, 2, 128).transpose(1, 0, 2, 3, 4)
        gf[4 * r:4 * r + 4] = np.asarray(o["o_gf"]).transpose(1, 0, 2, 3, 4)
        gb_[4 * r:4 * r + 4] = np.asarray(o["o_gb"]).transpose(1, 0, 2, 3, 4)
        df[4 * r:4 * r + 4] = np.asarray(o["o_df"]).transpose(1, 0, 2, 3, 4)
        db[4 * r:4 * r + 4] = np.asarray(o["o_db"]).transpose(1, 0, 2, 3, 4)
    return (y_prompt, y_sample, nk, nv, gf, gb_, df, db)
```
